# Optimizing a Trainium2 kernel written in Bass

```python
import jax, jax.numpy as jnp
from jax import lax
import numpy as np

D_MODEL = 1024
BATCH = 8
SEQ = 2048
DEPTH = 4
DEC_BATCH = 128
DEC_SEQ = 1
PAST_LEN = 16384
PAGE_SIZE = 128

MIX_WIDTH = D_MODEL
N_MIXERS = 4
GROUP_WIDTH = MIX_WIDTH // N_MIXERS
W_A = GROUP_WIDTH
W_B = GROUP_WIDTH
W_C = GROUP_WIDTH
W_D = GROUP_WIDTH
HEADS_PER_GROUP = 4
HEAD_DIM = GROUP_WIDTH // HEADS_PER_GROUP
POOL_WINDOWS = (2, 4, 8, 16)
POOL_BUF = max(POOL_WINDOWS) - 1
CONV_A_WIDTH = 3
CONV_C_WIDTH = 31
CONV_F_WIDTH = 3
CHUNK = 128
D_FF = 2816
IN_COLS = 3 * W_A + W_B + 2 * W_C + 2 * W_D
OFF_B = 3 * W_A
OFF_C = OFF_B + W_B
OFF_D = OFF_C + 2 * W_C
EPS = 1e-6

kernel_name = "hybrid_conv_pool_conformer_gmlp_decoder_step"


def _rmsnorm(x, g):
    xf = x.astype(jnp.float32)
    inv = lax.rsqrt(jnp.mean(xf * xf, axis=-1, keepdims=True) + EPS)
    return (xf * inv).astype(x.dtype) * g


def _layernorm(x, g, b):
    xf = x.astype(jnp.float32)
    mu = jnp.mean(xf, axis=-1, keepdims=True)
    xc = xf - mu
    var = jnp.mean(xc * xc, axis=-1, keepdims=True)
    return (xc * lax.rsqrt(var + EPS)).astype(x.dtype) * g + b


def _causal_dwconv(x, prefix, w):
    k = w.shape[0]
    xp = jnp.concatenate([prefix.astype(x.dtype), x], axis=1)
    y = lax.conv_general_dilated(
        xp, w[:, None, :].astype(x.dtype), window_strides=(1,), padding="VALID",
        dimension_numbers=("NWC", "WIO", "NWC"), feature_group_count=x.shape[-1])
    return y, xp[:, xp.shape[1] - (k - 1):]


def _pool_mixer(p, prefix, pos0, pool_w, pool_scale):
    b, l, c = p.shape
    xp = jnp.concatenate([prefix.astype(p.dtype), p], axis=1)
    cs = jnp.cumsum(xp.astype(jnp.float32), axis=1)
    cs = jnp.pad(cs, ((0, 0), (1, 0), (0, 0)))
    t = jnp.arange(l)
    means = []
    for g, win in enumerate(POOL_WINDOWS):
        sl = slice(g * HEAD_DIM, (g + 1) * HEAD_DIM)
        s = cs[:, POOL_BUF + 1:POOL_BUF + 1 + l, sl] - cs[:, POOL_BUF + 1 - win:POOL_BUF + 1 - win + l, sl]
        cnt = jnp.minimum(win, pos0 + t + 1).astype(jnp.float32)[:, None]
        means.append(s / cnt)
    pooled = jnp.concatenate(means, axis=-1).astype(p.dtype) - p
    y = jnp.einsum("blgc,gcd->blgd", pooled.reshape(b, l, HEADS_PER_GROUP, HEAD_DIM), pool_w)
    return y.reshape(b, l, c) * pool_scale, xp[:, xp.shape[1] - POOL_BUF:]


def _chunk_spatial_gate(u, v, w_s, b_s):
    b, l, c = v.shape
    lp = -(-l // CHUNK) * CHUNK
    vp = jnp.pad(v, ((0, 0), (0, lp - l), (0, 0))).reshape(b, lp // CHUNK, CHUNK, HEADS_PER_GROUP, HEAD_DIM)
    mask = jnp.tril(jnp.ones((CHUNK, CHUNK), dtype=bool))
    ws = jnp.where(mask[None], w_s, 0)
    s = jnp.einsum("hij,bnjhd->bnihd", ws, vp) + b_s.T[None, None, :, :, None]
    s = s.reshape(b, lp, c)[:, :l]
    return u * s


def _layer(x, pos0, buf_a, buf_pool, buf_c, buf_f,
           norm_mix, w_in, conv_a_w, pool_w, pool_scale, conv_c_w, conv_c_b,
           ln_c_g, ln_c_b, ln_d_g, ln_d_b, w_s, b_s, w_out,
           norm_ffn, w_up, conv_f_w, w_down):
    h = _rmsnorm(x, norm_mix)
    z = h @ w_in
    a_b = z[..., 0:W_A]
    a_c = z[..., W_A:2 * W_A]
    a_x = z[..., 2 * W_A:3 * W_A]
    conv_a, new_a = _causal_dwconv(a_c * a_x, buf_a, conv_a_w)
    y_a = a_b * conv_a
    p = z[..., OFF_B:OFF_B + W_B]
    y_b, new_pool = _pool_mixer(p, buf_pool, pos0, pool_w, pool_scale)
    c_in = z[..., OFF_C:OFF_C + 2 * W_C]
    glu = c_in[..., :W_C] * jax.nn.sigmoid(c_in[..., W_C:])
    conv_c, new_c = _causal_dwconv(glu, buf_c, conv_c_w)
    y_c = jax.nn.silu(_layernorm(conv_c + conv_c_b, ln_c_g, ln_c_b))
    d_in = jax.nn.gelu(z[..., OFF_D:OFF_D + 2 * W_D])
    u = d_in[..., :W_D]
    v = _layernorm(d_in[..., W_D:], ln_d_g, ln_d_b)
    y_d = _chunk_spatial_gate(u, v, w_s, b_s)
    x = x + jnp.concatenate([y_a, y_b, y_c, y_d], axis=-1) @ w_out
    up = _rmsnorm(x, norm_ffn) @ w_up
    up_c, new_f = _causal_dwconv(up, buf_f, conv_f_w)
    x = x + (jax.nn.silu(up_c[..., :D_FF]) * up_c[..., D_FF:]) @ w_down
    return x, new_a, new_pool, new_c, new_f, v


def setup_inputs(seed: int = 0) -> dict:
    key = jax.random.key(seed)
    k = jax.random.split(key, 26)

    def nrm(kk, shape, scale):
        return scale * jax.random.normal(kk, shape, jnp.float32)

    return {
        "x_prompt": nrm(k[0], (BATCH, SEQ, D_MODEL), 1.0),
        "x_sample": nrm(k[1], (DEC_BATCH, DEC_SEQ, D_MODEL), 1.0),
        "state_conv_a": nrm(k[2], (DEPTH, DEC_BATCH, CONV_A_WIDTH - 1, W_A), 1.0),
        "state_pool": nrm(k[3], (DEPTH, DEC_BATCH, POOL_BUF, W_B), 1.0),
        "state_conv_c": nrm(k[4], (DEPTH, DEC_BATCH, CONV_C_WIDTH - 1, W_C), 0.5),
        "state_conv_ffn": nrm(k[5], (DEPTH, DEC_BATCH, CONV_F_WIDTH - 1, 2 * D_FF), 1.0),
        "norm_mix": 1.0 + nrm(k[6], (DEPTH, D_MODEL), 0.02),
        "w_in": nrm(k[7], (DEPTH, D_MODEL, IN_COLS), D_MODEL ** -0.5),
        "conv_a_w": nrm(k[8], (DEPTH, CONV_A_WIDTH, W_A), CONV_A_WIDTH ** -0.5),
        "pool_w": nrm(k[9], (DEPTH, HEADS_PER_GROUP, HEAD_DIM, HEAD_DIM), HEAD_DIM ** -0.5),
        "pool_scale": 1.0 + nrm(k[10], (DEPTH, W_B), 0.1),
        "conv_c_w": nrm(k[11], (DEPTH, CONV_C_WIDTH, W_C), CONV_C_WIDTH ** -0.5),
        "conv_c_b": nrm(k[12], (DEPTH, W_C), 0.02),
        "ln_c_g": 1.0 + nrm(k[13], (DEPTH, W_C), 0.02),
        "ln_c_b": nrm(k[14], (DEPTH, W_C), 0.02),
        "ln_d_g": 1.0 + nrm(k[15], (DEPTH, W_D), 0.02),
        "ln_d_b": nrm(k[16], (DEPTH, W_D), 0.02),
        "w_s": nrm(k[17], (DEPTH, HEADS_PER_GROUP, CHUNK, CHUNK), CHUNK ** -0.5),
        "b_s": 1.0 + nrm(k[18], (DEPTH, HEADS_PER_GROUP, CHUNK), 0.1),
        "w_out": nrm(k[19], (DEPTH, MIX_WIDTH, D_MODEL), MIX_WIDTH ** -0.5),
        "norm_ffn": 1.0 + nrm(k[20], (DEPTH, D_MODEL), 0.02),
        "w_up": nrm(k[21], (DEPTH, D_MODEL, 2 * D_FF), D_MODEL ** -0.5),
        "conv_f_w": nrm(k[22], (DEPTH, CONV_F_WIDTH, 2 * D_FF), CONV_F_WIDTH ** -0.5),
        "w_down": nrm(k[23], (DEPTH, D_FF, D_MODEL), D_FF ** -0.5),
        "norm_final": 1.0 + nrm(k[24], (D_MODEL,), 0.02),
    }


def reference(x_prompt, x_sample, state_conv_a, state_pool, state_conv_c, state_conv_ffn,
              norm_mix, w_in, conv_a_w, pool_w, pool_scale, conv_c_w, conv_c_b,
              ln_c_g, ln_c_b, ln_d_g, ln_d_b, w_s, b_s, w_out,
              norm_ffn, w_up, conv_f_w, w_down, norm_final):
    dt = x_prompt.dtype
    zero_a = jnp.zeros((BATCH, CONV_A_WIDTH - 1, W_A), dt)
    zero_pool = jnp.zeros((BATCH, POOL_BUF, W_B), dt)
    zero_c = jnp.zeros((BATCH, CONV_C_WIDTH - 1, W_C), dt)
    zero_f = jnp.zeros((BATCH, CONV_F_WIDTH - 1, 2 * D_FF), dt)

    xp, xs = x_prompt, x_sample
    pa_l, pp_l, pc_l, pf_l = [], [], [], []
    sa_l, sp_l, sc_l, sf_l, sv_l = [], [], [], [], []
    for l in range(DEPTH):
        params = (norm_mix[l], w_in[l], conv_a_w[l], pool_w[l], pool_scale[l], conv_c_w[l], conv_c_b[l],
                  ln_c_g[l], ln_c_b[l], ln_d_g[l], ln_d_b[l], w_s[l], b_s[l], w_out[l],
                  norm_ffn[l], w_up[l], conv_f_w[l], w_down[l])
        xp, pa, pp, pc, pf, _ = _layer(xp, 0, zero_a, zero_pool, zero_c, zero_f, *params)
        xs, sa, sp, sc, sf, sv = _layer(xs, PAST_LEN, state_conv_a[l], state_pool[l],
                                        state_conv_c[l], state_conv_ffn[l], *params)
        pa_l.append(pa); pp_l.append(pp); pc_l.append(pc); pf_l.append(pf)
        sa_l.append(sa); sp_l.append(sp); sc_l.append(sc); sf_l.append(sf); sv_l.append(sv)

    y_prompt = _rmsnorm(xp, norm_final)
    y_sample = _rmsnorm(xs, norm_final)
    return (y_prompt, y_sample,
            jnp.stack(pa_l), jnp.stack(pp_l), jnp.stack(pc_l), jnp.stack(pf_l),
            jnp.stack(sa_l), jnp.stack(sp_l), jnp.stack(sc_l), jnp.stack(sf_l), jnp.stack(sv_l))
```

```python
import numpy as np
from contextlib import ExitStack
import concourse.bass as bass
import concourse.mybir as mybir
from concourse.bass_utils import run_bass_kernel_spmd

F32 = mybir.dt.float32
BF16 = mybir.dt.bfloat16
AF = mybir.ActivationFunctionType
ALU = mybir.AluOpType
AX = mybir.AxisListType

NCORES = 8
D = 1024
SEQ = 2048
DEPTH = 4
NS = 16
DFF = 2816
NFC = 44
SEGW = 1024
TT = SEQ + NS
EPS = 1e-6
WINS = (2, 4, 8, 16)

C_NM, C_NF, C_CAW, C_PSC, C_CCW, C_CCB, C_LCG, C_LCB, C_LDG, C_LDB, C_CFW = 0, 8, 16, 22, 24, 86, 88, 90, 92, 94, 96
NPAR = 228


class Res:
    __slots__ = ("name", "w", "rs")

    def __init__(self, name):
        self.name = name
        self.w = None
        self.rs = {}


class Sched:
    K = 8

    def __init__(self, nc, stack):
        self.nc = nc
        self.lists = {e: [] for e in ("pe", "act", "dve", "pool", "sp")}
        self.sem = {e: stack.enter_context(nc.semaphore("c_" + e)) for e in ("pe", "act", "dve", "pool")}
        self.cnt = {e: 0 for e in self.sem}
        self.seen = {e: {} for e in self.lists}
        self.dq = {q: [stack.enter_context(nc.semaphore("d_%s%d" % (q, i))) for i in range(self.K)] for q in ("sp", "pool", "act")}
        self.dcnt = {"sp": 0, "pool": 0, "act": 0}
        self.semobj = {}

    def _deps(self, reads, writes):
        deps = []
        for r in reads:
            if r.w is not None:
                deps.append(r.w)
        for w in writes:
            if w.w is not None:
                deps.append(w.w)
            deps.extend(w.rs.values())
        return deps

    def _waits(self, eng, deps):
        for (sem, val, src) in deps:
            if src == "pe" and eng == "pe":
                continue
            key = id(sem)
            if self.seen[eng].get(key, 0) >= val:
                continue
            self.seen[eng][key] = val
            self.lists[eng].append(lambda e, sem=sem, val=val: e.wait_ge(sem, val))

    def _update(self, tok, reads, writes):
        for r in reads:
            key = id(tok[0])
            old = r.rs.get(key)
            if old is None or old[1] < tok[1]:
                r.rs[key] = tok
        for w in writes:
            w.w = tok
            w.rs = {}

    def op(self, eng, fn, reads=(), writes=()):
        self._waits(eng, self._deps(reads, writes))
        self.cnt[eng] += 1
        sem = self.sem[eng]
        tok = (sem, self.cnt[eng], eng)
        self.lists[eng].append(lambda e, fn=fn, sem=sem: fn(e).then_inc(sem, 1))
        self._update(tok, reads, writes)

    def dma(self, q, fn, reads=(), writes=()):
        deps = self._deps(reads, writes)
        i = self.dcnt[q]
        self.dcnt[q] += 1
        sem = self.dq[q][i % self.K]
        if i >= self.K:
            deps.append((sem, 16 * (i // self.K), "dma"))
        self._waits(q, deps)
        tok = (sem, 16 * (i // self.K + 1), "dma")
        self.lists[q].append(lambda e, fn=fn, sem=sem: fn(e).then_inc(sem, 16))
        self._update(tok, reads, writes)

    def transfer(self, olds, news):
        toks = {}
        for o in olds:
            for t in ([o.w] if o.w is not None else []) + list(o.rs.values()):
                key = id(t[0])
                if key not in toks or toks[key][1] < t[1]:
                    toks[key] = t
        for n in news:
            n.w = None
            n.rs = dict(toks)

    def finish(self):
        for q in ("sp", "pool", "act"):
            n = self.dcnt[q]
            for k in range(self.K):
                uses = (n - k + self.K - 1) // self.K if n > k else 0
                if uses > 0:
                    sem = self.dq[q][k]
                    self.lists["sp"].append(lambda e, sem=sem, v=16 * uses: e.wait_ge(sem, v))


def build_nc():
    nc = bass.Bass("TRN2", target_bir_lowering=False)

    def din(name, shape):
        return nc.dram_tensor(name, list(shape), F32, kind="ExternalInput").ap()

    def dout(name, shape):
        return nc.dram_tensor(name, list(shape), F32, kind="ExternalOutput").ap()

    xp_d = din("x_prompt", (SEQ, D))
    xs_d = din("x_sample", (NS, D))
    sta_d = din("state_conv_a", (DEPTH, NS, 2, 256))
    stp_d = din("state_pool", (DEPTH, NS, 15, 256))
    stc_d = din("state_conv_c", (DEPTH, NS, 30, 256))
    stf_d = din("state_conv_ffn", (DEPTH, NS, 2, 2 * DFF))
    norm_mix_d = din("norm_mix", (DEPTH, D))
    w_in_d = din("w_in", (DEPTH, D, 2048))
    conv_a_w_d = din("conv_a_w", (DEPTH, 3, 256))
    pool_w_d = din("pool_w", (DEPTH, 4, 64, 64))
    pool_scale_d = din("pool_scale", (DEPTH, 256))
    conv_c_w_d = din("conv_c_w", (DEPTH, 31, 256))
    conv_c_b_d = din("conv_c_b", (DEPTH, 256))
    ln_c_g_d = din("ln_c_g", (DEPTH, 256))
    ln_c_b_d = din("ln_c_b", (DEPTH, 256))
    ln_d_g_d = din("ln_d_g", (DEPTH, 256))
    ln_d_b_d = din("ln_d_b", (DEPTH, 256))
    w_s_d = din("w_s", (DEPTH, 4, 128, 128))
    b_s_d = din("b_s", (DEPTH, 4, 128))
    w_out_d = din("w_out", (DEPTH, D, D))
    norm_ffn_d = din("norm_ffn", (DEPTH, D))
    w_up_d = din("w_up", (DEPTH, D, 2 * DFF))
    conv_f_w_d = din("conv_f_w", (DEPTH, 3, 2 * DFF))
    w_down_d = din("w_down", (DEPTH, DFF, D))
    norm_final_d = din("norm_final", (D,))

    yp_o = dout("y_prompt", (SEQ, D))
    ys_o = dout("y_sample", (NS, D))
    pa_o = dout("new_conv_a_prompt", (DEPTH, 2, 256))
    pp_o = dout("new_pool_prompt", (DEPTH, 15, 256))
    pc_o = dout("new_conv_c_prompt", (DEPTH, 30, 256))
    pf_o = dout("new_conv_ffn_prompt", (DEPTH, 2, 2 * DFF))
    sa_o = dout("new_conv_a_sample", (DEPTH, NS, 2, 256))
    sp_o = dout("new_pool_sample", (DEPTH, NS, 15, 256))
    sc_o = dout("new_conv_c_sample", (DEPTH, NS, 30, 256))
    sf_o = dout("new_conv_ffn_sample", (DEPTH, NS, 2, 2 * DFF))
    sv_o = dout("new_chunk_v_sample", (DEPTH, NS, 256))

    stack = ExitStack()
    with stack:
        def sb(name, shape, dt=F32):
            return stack.enter_context(nc.sbuf_tensor(name, list(shape), dt))

        def ps(name, shape, dt=F32):
            return stack.enter_context(nc.psum_tensor(name, list(shape), dt))

        S = Sched(nc, stack)

        X = sb("X", (128, 8, TT))
        H = sb("H", (128, 8, 1040), BF16)
        ARENA = sb("ARENA", (128, 22 * 1040), BF16)
        RING = [sb("RING%d" % i, (128, 4096), BF16) for i in range(3)]
        NSCR = 5
        SCRW = 1072
        SCR = [sb("SCR%d" % i, (128, SCRW)) for i in range(NSCR)]
        PARAM = sb("PARAM", (128, DEPTH, NPAR))
        STG = sb("STG", (128, 1024))
        PSTG = STG[:, 0:256].rearrange("p (a b) -> p a b", a=2)
        PSTG2 = STG[:, 256:384]
        DIAG = sb("DIAG", (128, 31, 128), BF16)
        WG = sb("WG", (128, 2, 128), BF16)
        WST = sb("WST", (128, 4, 128), BF16)
        BSB = sb("BSB", (128, 2, 128))
        WS00 = sb("WS00", (128, 2))
        STA = sb("STA", (128, 2, NS, 2))
        STP = sb("STP", (128, 2, NS, 15))
        STC = sb("STC", (128, 2, NS, 30))
        STF = sb("STF", (128, NFC, NS, 2))
        CA_T = sb("CA_T", (128, 2, 2))
        P_T = sb("P_T", (128, 2, 15))
        GLU_T = sb("GLU_T", (128, 2, 30))
        UP_T = sb("UP_T", (128, 2, NFC))
        UPS = sb("UPS", (128, NFC, NS))
        NEWS = sb("NEWS", (128, 4, 2, NS))
        IDF = sb("IDF", (128, 128))
        IDB = sb("IDB", (128, 128), BF16)
        ONESB = sb("ONESB", (128, 128), BF16)
        MASKT = sb("MASKT", (128, 128))
        IOTA_I = sb("IOTA_I", (128, 128))
        PIDX = sb("PIDX", (128, 1))
        EPS_T = sb("EPS_T", (128, 1))
        SQD = sb("SQD", (128, 1))
        CORR = sb("CORR", (128, 2, 16))
        NFIN = sb("NFIN", (128, 8))
        OSTG = [sb("OSTG%d" % i, (128, 128)) for i in range(2)]
        SMALL = sb("SMALL", (128, 8, NS))

        PA = ps("PA", (128, 1536))
        PB = ps("PB", (128, 1536))
        PC = ps("PC", (128, 512))
        PD = ps("PD", (128, 512))

        Y = ARENA[:, 0:8 * 1040].rearrange("p (c t) -> p c t", c=8)
        o = 8 * 1040
        GLUB = ARENA[:, o:o + 2 * 1072].rearrange("p (c t) -> p c t", c=2)
        o += 2 * 1072
        POOLB = ARENA[:, o:o + 2 * 1040].rearrange("p (c t) -> p c t", c=2)
        VB = POOLB
        o += 2 * 1040
        ASCR = []
        ascr_offs = []
        for i in range(4):
            ASCR.append(ARENA[:, o:o + 2 * SCRW].bitcast(F32))
            ascr_offs.append(o)
            o += 2 * SCRW
        assert o <= 22 * 1040, o
        ACTB = ARENA[:, 0:22 * 1040].rearrange("p (c t) -> p c t", c=22)
        SQ = ARENA[:, 0:8 * 1040].rearrange("p (c t) -> p c t", c=8)

        rXs = [[Res("X%d_%d" % (c, sg)) for c in range(8)] for sg in range(2)]
        rX = rXs[0] + rXs[1]
        rXhs = [[[Res("Xh%d_%d_%d" % (sg, c, h)) for h in range(2)] for c in range(8)] for sg in range(2)]
        rHch = [[Res("H%d_%d" % (c, h)) for h in range(2)] for c in range(8)]
        rHc = [r for pair in rHch for r in pair]
        rH0 = [rHch[c][0] for c in range(8)]
        rH1 = [rHch[c][1] for c in range(8)]
        rSQc = [Res("SQ%d" % c) for c in range(8)]
        rYc = [Res("Y%d" % c) for c in range(8)]
        rGLUBc = [Res("GLUB0"), Res("GLUB1")]
        rPOOLB = Res("POOLB")
        rVB = rPOOLB
        rASCR = [Res("ASCR%d" % i) for i in range(4)]
        rACTBc = [Res("ACTB%d" % j) for j in range(22)]
        rSQ = Res("SQ")
        rY45 = [[Res("Y%d_h%d" % (4 + c, h)) for h in range(2)] for c in range(2)]
        arena_mix = rYc + rGLUBc + [rPOOLB] + rASCR + rY45[0] + rY45[1]
        rRING = [Res("RING%d" % i) for i in range(3)]
        rSCR = [Res("SCR%d" % i) for i in range(NSCR)]
        rPA, rPB, rPC, rPD = Res("PA"), Res("PB"), Res("PC"), Res("PD")
        rPARAM, rPSTG, rDIAG, rWG, rWSF, rWST, rBSB, rWS00 = (Res(n) for n in ("PARAM", "PSTG", "DIAG", "WG", "WSF", "WST", "BSB", "WS00"))
        rSTA, rSTP, rSTC, rSTF = Res("STA"), Res("STP"), Res("STC"), Res("STF")
        rUP_T, rNEWS = Res("UP_T"), Res("NEWS")
        rCA_Tc = [Res("CA_T0"), Res("CA_T1")]
        rP_Tc = [Res("P_T0"), Res("P_T1")]
        rGLU_Tc = [Res("GLU_T0"), Res("GLU_T1")]
        rUPSc = [Res("UPS%d" % i) for i in range(NFC)]
        rCONST = Res("CONST")
        rSQD = Res("SQD")
        rPARAMl = [Res("PARAM%d" % i) for i in range(DEPTH)]
        rR2 = Res("R2")
        rOSTG = [Res("OSTG0"), Res("OSTG1")]
        rSMALL = Res("SMALL")
        rNFIN = Res("NFIN")
        rDRAM = Res("DRAM")

        scr_all = [(SCR[i][:, :], rSCR[i]) for i in range(NSCR)]
        scr_mix = scr_all + [(ASCR[i], rASCR[i]) for i in range(4)]

        ostg_i = [0]

        S.op("pool", lambda e: e.iota(IOTA_I[:, :], [[1, 128]], base=0, channel_multiplier=0, allow_small_or_imprecise_dtypes=True), writes=[rCONST])
        S.op("pool", lambda e: e.iota(PIDX[:, :], [[0, 1]], base=0, channel_multiplier=1, allow_small_or_imprecise_dtypes=True), writes=[rCONST])
        S.op("pool", lambda e: e.memset(ONESB[:, :], 1.0), writes=[rCONST])
        S.op("pool", lambda e: e.memset(EPS_T[:, :], EPS), writes=[rCONST])
        S.op("pool", lambda e: e.memset(WG[:, :, :], 0.0), writes=[rWG])
        S.op("dve", lambda e: e.tensor_scalar(out=IDF[:, :], in0=IOTA_I[:, :], scalar1=PIDX[:, 0:1], scalar2=None, op0=ALU.is_equal), reads=[rCONST], writes=[rCONST])
        S.op("dve", lambda e: e.tensor_copy(out=IDB[:, :], in_=IDF[:, :]), reads=[rCONST], writes=[rCONST])
        S.op("dve", lambda e: e.tensor_scalar(out=MASKT[:, :], in0=IOTA_I[:, :], scalar1=PIDX[:, 0:1], scalar2=None, op0=ALU.is_ge), reads=[rCONST], writes=[rCONST])
        for c in range(2):
            for hf in range(2):
                win = float(WINS[2 * c + hf])
                S.op("dve", lambda e, c=c, hf=hf, win=win: e.tensor_scalar(out=CORR[hf * 64:(hf + 1) * 64, c, :], in0=IOTA_I[hf * 64:(hf + 1) * 64, 0:16],
                                                                        scalar1=1.0, scalar2=win, op0=ALU.add, op1=ALU.min), reads=[rCONST], writes=[rCONST])
        S.op("dve", lambda e: e.reciprocal(out=CORR[:, :, :], in_=CORR[:, :, :]), reads=[rCONST], writes=[rCONST])

        def load_params_dma(l, q="sp"):
            rows = [
                (norm_mix_d[l].rearrange("(c p) -> c p", p=128), 8),
                (norm_ffn_d[l].rearrange("(c p) -> c p", p=128), 8),
                (conv_a_w_d[l].rearrange("k (c p) -> (k c) p", p=128), 6),
                (pool_scale_d[l].rearrange("(c p) -> c p", p=128), 2),
                (conv_c_w_d[l].rearrange("k (c p) -> (k c) p", p=128), 62),
                (conv_c_b_d[l].rearrange("(c p) -> c p", p=128), 2),
                (ln_c_g_d[l].rearrange("(c p) -> c p", p=128), 2),
                (ln_c_b_d[l].rearrange("(c p) -> c p", p=128), 2),
                (ln_d_g_d[l].rearrange("(c p) -> c p", p=128), 2),
                (ln_d_b_d[l].rearrange("(c p) -> c p", p=128), 2),
            ]
            r0 = 0
            for (ap, n) in rows:
                S.dma(q, lambda e, ap=ap, r0=r0, n=n: e.dma_start(out=PSTG[r0:r0 + n, 0, :], in_=ap), writes=[rPSTG])
                r0 += n
            cf = conv_f_w_d[l].rearrange("k (c p) -> (k c) p", p=128)
            S.dma(q, lambda e: e.dma_start(out=PSTG[0:128, 1, :], in_=cf[0:128, :]), writes=[rPSTG])
            S.dma(q, lambda e: e.dma_start(out=PSTG2[0:4, :], in_=cf[128:132, :]), writes=[rPSTG])

        def load_params_compute(l, ps=None):
            PX, rPX = ps if ps is not None else (PD, rPD)
            S.op("pe", lambda e: e.transpose(PX[:, 0:96], PSTG[0:96, 0, :], IDF[0:96, 0:96]), reads=[rPSTG, rCONST], writes=[rPX])
            S.op("pe", lambda e: e.transpose(PX[:, 96:224], PSTG[0:128, 1, :], IDF[:, :]), reads=[rPSTG, rCONST], writes=[rPX])
            S.op("pe", lambda e: e.transpose(PX[:, 224:228], PSTG2[0:4, :], IDF[0:4, 0:4]), reads=[rPSTG, rCONST], writes=[rPX])
            S.op("act", lambda e: e.activation(out=PARAM[:, l, :], in_=PX[:, 0:NPAR], func=AF.Copy), reads=[rPX], writes=[rPARAMl[l]])

        def load_params(l):
            load_params_dma(l)
            load_params_compute(l)

        def par(l, col):
            return PARAM[:, l, col:col + 1]


        for tb in range(17):
            ntok = 128 if tb < 16 else NS
            src = xp_d[tb * 128:(tb + 1) * 128, :] if tb < 16 else xs_d[:, :]
            sap, sres = scr_all[tb % NSCR]
            if tb == NSCR:
                load_params_dma(0, q="sp")
            S.dma("sp", lambda e, sap=sap, src=src, ntok=ntok: e.dma_start(out=sap[0:ntok, 0:1024], in_=src), writes=[sres])
            pt, rpt = (PA, rPA) if tb % 2 == 0 else (PB, rPB)

            def tr(e, sap=sap, pt=pt, ntok=ntok):
                for c in range(8):
                    ins = e.transpose(pt[:, c * 128:c * 128 + ntok], sap[0:ntok, c * 128:(c + 1) * 128], IDF[0:ntok, 0:ntok])
                return ins
            S.op("pe", tr, reads=[sres, rCONST], writes=[rpt])
            eng = "act" if tb % 2 == 0 else "dve"
            pv = pt[:, 0:1024].rearrange("p (c t) -> p c t", c=8)[:, :, 0:ntok]
            if eng == "act":
                S.op("act", lambda e, pv=pv, tb=tb, ntok=ntok: e.activation(out=X[:, :, tb * 128:tb * 128 + ntok], in_=pv, func=AF.Copy), reads=[rpt], writes=rX)
            else:
                S.op("dve", lambda e, pv=pv, tb=tb, ntok=ntok: e.tensor_copy(out=X[:, :, tb * 128:tb * 128 + ntok], in_=pv), reads=[rpt], writes=rX)

        load_params_compute(0)
        S.dma("sp", lambda e: e.dma_start(out=PSTG[0:8, 0, :], in_=norm_final_d.rearrange("(c p) -> c p", p=128)), writes=[rPSTG])
        S.op("pe", lambda e: e.transpose(PD[:, 0:8], PSTG[0:8, 0, :], IDF[0:8, 0:8]), reads=[rPSTG, rCONST], writes=[rPD])
        S.op("act", lambda e: e.activation(out=NFIN[:, :], in_=PD[:, 0:8], func=AF.Copy), reads=[rPD], writes=[rNFIN])

        ring_i = [0]

        def ring_load(parts):
            i = ring_i[0] % 3
            ring_i[0] += 1
            slot, res = RING[i], rRING[i]
            for (dst_fn, src) in parts:
                S.dma("pool", lambda e, dst=dst_fn(slot), src=src: e.dma_start(out=dst, in_=src), writes=[res])
            return slot, res

        psl_i = [0]

        def next_pslot():
            i = psl_i[0] % 2
            psl_i[0] += 1
            return (PA, rPA) if i == 0 else (PB, rPB)

        def seg_tiles(seg):
            t = [(0, 512), (512, 512)]
            if seg == 1:
                t.append((1024, NS))
            return t

        def mm_chunk(pt, rpt, lhs_fn, nk, rhs_fn, tiles, reads, tile_reads=None):
            def mk(tl):
                def f(e):
                    ins = None
                    for (c0, n) in tl:
                        for k in range(nk):
                            ins = e.matmul(pt[:, c0:c0 + n], lhsT=lhs_fn(k), rhs=rhs_fn(k, c0, n), start=(k == 0), stop=(k == nk - 1))
                    return ins
                return f
            if tile_reads is None:
                S.op("pe", mk(tiles), reads=reads, writes=[rpt])
            else:
                for ti, tl in enumerate(tiles):
                    S.op("pe", mk([tl]), reads=reads + tile_reads[ti], writes=[rpt])

        def tr_out(src_ap, n, dst_ap, reads, eng="act"):
            i = ostg_i[0] % 2
            ostg_i[0] += 1
            st, rst = OSTG[i], rOSTG[i]
            PX, rPX = PC, rPC
            S.op("pe", lambda e: e.transpose(PX[0:n, 0:128], src_ap, IDF[:, :]), reads=reads + [rCONST], writes=[rPX])
            if eng == "act":
                S.op("act", lambda e: e.activation(out=st[0:n, :], in_=PX[0:n, 0:128], func=AF.Copy), reads=[rPX], writes=[rst])
            else:
                S.op("dve", lambda e: e.tensor_copy(out=st[0:n, :], in_=PX[0:n, 0:128]), reads=[rPX], writes=[rst])
            S.dma("sp", lambda e: e.dma_start(out=dst_ap, in_=st[0:n, :]), reads=[rst])

        rSTGO = [Res("STGO0"), Res("STGO1")]
        stgo_i = [0]

        def tr_out_multi(src_aps, n, dsts, reads):
            k = len(src_aps)
            st = STG[0:n, 512:512 + k * 128]
            rst = rSTGO[0]
            PX, rPX = PD, rPD

            def trm(e):
                ins = None
                for j, sap in enumerate(src_aps):
                    ins = e.transpose(PX[0:n, j * 128:(j + 1) * 128], sap, IDF[:, :])
                return ins
            S.op("pe", trm, reads=reads + [rCONST], writes=[rPX])
            S.op("act", lambda e: e.activation(out=st, in_=PX[0:n, 0:k * 128], func=AF.Copy), reads=[rPX], writes=[rst])
            for (dap, c0, ncol) in dsts:
                S.dma("sp", lambda e, dap=dap, c0=c0, ncol=ncol: e.dma_start(out=dap, in_=STG[0:n, 512 + c0:512 + c0 + ncol]), reads=[rst])

        rms_state = {}

        def rms_stats(seg, W, xoff, sql, Rap, rRh, phase="all"):
            if phase in ("sq", "all"):
                S.op("act", lambda e: e.activation(out=SQD[:, 0:1], in_=EPS_T[:, 0:1], func=AF.Sqrt), reads=[rCONST], writes=[rSQD])
                for c in range(8):
                    S.op("act", lambda e, c=c: e.activation(out=sql[c][0][:, 0:W], in_=X[:, c, xoff:xoff + W], func=AF.Square), reads=[rXs[seg][c]], writes=[sql[c][1]])
            if phase == "sq":
                return
            pt, rpt = next_pslot()
            tl = seg_tiles(seg)

            def f1(e):
                ins = None
                for (c0, n) in tl:
                    for k in range(7):
                        ins = e.matmul(pt[:, c0:c0 + n], lhsT=ONESB[:, :], rhs=sql[k][0][:, c0:c0 + n], start=(k == 0), stop=False)
                return ins

            def f2(e):
                ins = None
                for (c0, n) in tl:
                    ins = e.matmul(pt[:, c0:c0 + n], lhsT=ONESB[:, :], rhs=sql[7][0][:, c0:c0 + n], start=False, stop=True)
                return ins
            S.op("pe", f1, reads=list(dict.fromkeys([sql[k][1] for k in range(7)])) + [rCONST], writes=[rpt])
            S.op("pe", f2, reads=[sql[7][1], rCONST], writes=[rpt])
            S.op("act", lambda e: e.activation(out=Rap[:, 0:W], in_=pt[:, 0:W], func=AF.Sqrt, bias=EPS_T[:, 0:1], scale=1.0 / D), reads=[rpt, rCONST], writes=rRh)

        def sq_bufs(bufs):
            out = []
            for (ap, r) in bufs:
                b = ap.bitcast(BF16)
                out.append((b[:, 0:SCRW], r))
                out.append((b[:, SCRW:2 * SCRW], r))
            return out

        def norm_apply(W, xoff, Rap, rRh, gcol, l, defer=False):
            seg = xoff // SEGW
            dq = []
            for h, (a, b) in enumerate(((0, 512), (512, W))):
                dq.append(lambda h=h, a=a, b=b: S.op("dve", lambda e: e.reciprocal(out=Rap[:, a:b], in_=Rap[:, a:b]), reads=[rRh[h]], writes=[rRh[h]]))
                for c in range(8):
                    dq.append(lambda h=h, a=a, b=b, c=c: S.op("dve", lambda e: e.scalar_tensor_tensor(out=H[:, c, a:b], in0=X[:, c, xoff + a:xoff + b], scalar=par(l, gcol + c), in1=Rap[:, a:b],
                                                                                                     op0=ALU.mult, op1=ALU.mult), reads=[rXs[seg][c], rRh[h], rPARAMl[l]], writes=[rHch[c][h]]))
            if defer:
                return dq
            for f_ in dq:
                f_()
            return []

        def ln_prep(src, c, W, tmpb, rtl, rcp=None):
            S.op("act", lambda e: e.activation(out=tmpb[:, c, 0:W], in_=src[0][:, 0:W], func=AF.Copy), reads=[src[1]], writes=rtl + ([rcp[c]] if rcp else []))
            S.op("act", lambda e: e.activation(out=tmpb[:, 2 + c, 0:W], in_=src[0][:, 0:W], func=AF.Square), reads=[src[1]], writes=rtl)

        def sqrt_preload():
            S.op("act", lambda e: e.activation(out=SQD[:, 0:1], in_=EPS_T[:, 0:1], func=AF.Sqrt), reads=[rCONST], writes=[rSQD])

        def ln_stats(srcs, W, seg, tmpb, rtmpb, mean_ap, rmean, rstd_ap, rrstd, prep_done=False, do_recip=True, rcp=None):
            rtl = rtmpb if isinstance(rtmpb, list) else [rtmpb]
            if not prep_done:
                sqrt_preload()
                for c in range(2):
                    ln_prep(srcs[c], c, W, tmpb, rtl)
            pt, rpt = next_pslot()
            mm_chunk(pt, rpt, lambda k: ONESB[:, :], 2, lambda k, c0, n: tmpb[:, k, c0:c0 + n], seg_tiles(seg), (list(rcp) if rcp else rtl) + [rCONST])
            S.op("act", lambda e: e.activation(out=mean_ap[:, 0:W], in_=pt[:, 0:W], func=AF.Copy, scale=1.0 / 256), reads=[rpt], writes=[rmean])
            pt2, rpt2 = next_pslot()
            mm_chunk(pt2, rpt2, lambda k: ONESB[:, :], 2, lambda k, c0, n: tmpb[:, 2 + k, c0:c0 + n], seg_tiles(seg), rtl + [rCONST])
            S.op("dve", lambda e: e.tensor_tensor(out=rstd_ap[:, 0:W], in0=mean_ap[:, 0:W], in1=mean_ap[:, 0:W], op=ALU.mult), reads=[rmean], writes=[rrstd])
            S.op("dve", lambda e: e.scalar_tensor_tensor(out=rstd_ap[:, 0:W], in0=pt2[:, 0:W], scalar=1.0 / 256, in1=rstd_ap[:, 0:W], op0=ALU.mult, op1=ALU.subtract),
                 reads=[rpt2, rrstd], writes=[rrstd])
            S.op("act", lambda e: e.activation(out=rstd_ap[:, 0:W], in_=rstd_ap[:, 0:W], func=AF.Sqrt, bias=EPS_T[:, 0:1], scale=1.0), reads=[rrstd, rCONST], writes=[rrstd])
            if do_recip:
                S.op("dve", lambda e: e.reciprocal(out=rstd_ap[:, 0:W], in_=rstd_ap[:, 0:W]), reads=[rrstd], writes=[rrstd])

        rSTG = rPSTG

        def prep_layer_dma(l, with_wsf=True):
            for c in range(2):
                for hf in range(2):
                    S.dma("pool", lambda e, c=c, hf=hf: e.dma_start(out=WG[hf * 64:(hf + 1) * 64, c, hf * 64:(hf + 1) * 64], in_=pool_w_d[l, 2 * c + hf, :, :]), writes=[rWG])
            if with_wsf:
                WSF = STG[:, 0:512].rearrange("p (h j) -> p h j", h=4)
                S.dma("sp", lambda e: e.dma_start(out=WSF, in_=w_s_d[l].rearrange("h i j -> i h j")), writes=[rSTG])
            for c in range(2):
                for hf in range(2):
                    hd = 2 * c + hf
                    S.dma("sp", lambda e, c=c, hf=hf, hd=hd: e.dma_start(out=BSB[hf * 64:(hf + 1) * 64, c, :], in_=b_s_d[l, hd, :].partition_broadcast(64)), writes=[rBSB])
                    S.dma("sp", lambda e, c=c, hf=hf, hd=hd: e.dma_start(out=WS00[hf * 64:(hf + 1) * 64, c:c + 1], in_=w_s_d[l, hd, 0, 0:1].partition_broadcast(64)), writes=[rWS00])

        def wsf_dma_hi(l):
            WSF = STG[:, 512:1024].rearrange("p (h j) -> p h j", h=4)
            S.dma("sp", lambda e: e.dma_start(out=WSF, in_=w_s_d[l].rearrange("h i j -> i h j")), writes=[rSTGO[0], rSTGO[1]])

        def prep_layer_compute(l, hi=False):
            WSF = (STG[:, 512:1024] if hi else STG[:, 0:512]).rearrange("p (h j) -> p h j", h=4)
            rsrc = [rSTGO[0], rSTGO[1]] if hi else [rSTG]
            for hd in range(4):
                S.op("pe", lambda e, hd=hd: e.transpose(PD[:, hd * 128:(hd + 1) * 128], WSF[:, hd, :], IDF[:, :]), reads=rsrc + [rCONST], writes=[rPD])
            S.op("dve", lambda e: e.tensor_tensor(out=WST[:, :, :], in0=PD[:, 0:512].rearrange("p (h i) -> p h i", h=4),
                                                  in1=MASKT[:, :].unsqueeze(1).to_broadcast([128, 4, 128]), op=ALU.mult), reads=[rPD, rCONST], writes=[rWST])

        def prep_layer(l):
            prep_layer_dma(l)
            prep_layer_compute(l)

        def state_batches(l):
            batches = []

            def b_sta():
                S.dma("sp", lambda e: e.dma_start(out=STG[0:32, 0:256], in_=sta_d[l].rearrange("s r c -> (s r) c")), writes=[rSTG])
                for c in range(2):
                    S.op("pe", lambda e, c=c: e.transpose(PC[:, c * 32:(c + 1) * 32], STG[0:32, c * 128:(c + 1) * 128], IDF[0:32, 0:32]), reads=[rSTG, rCONST], writes=[rPC])
                S.op("act", lambda e: e.activation(out=STA[:, :, :, :].rearrange("p c s r -> p (c s r)"), in_=PC[:, 0:64], func=AF.Copy), reads=[rPC], writes=[rSTA])
            batches.append(b_sta)

            def b_stp():
                stp_rows = stp_d[l].rearrange("s r c -> (s r) c")
                for t in range(2):
                    S.dma("sp", lambda e, t=t: e.dma_start(out=STG[0:120, t * 256:(t + 1) * 256], in_=stp_rows[t * 120:(t + 1) * 120, :]), writes=[rSTG])
                for t in range(2):
                    for c in range(2):
                        S.op("pe", lambda e, t=t, c=c: e.transpose(PC[:, c * 240 + t * 120:c * 240 + (t + 1) * 120], STG[0:120, t * 256 + c * 128:t * 256 + (c + 1) * 128], IDF[0:120, 0:120]),
                             reads=[rSTG, rCONST], writes=[rPC])
                S.op("act", lambda e: e.activation(out=STP[:, :, :, :].rearrange("p c s r -> p (c s r)"), in_=PC[:, 0:480], func=AF.Copy), reads=[rPC], writes=[rSTP])
            batches.append(b_stp)

            def mk_stc(c):
                def b_stc():
                    stc_rows = stc_d[l].rearrange("s r c -> (s r) c")
                    for t in range(4):
                        S.dma("sp", lambda e, t=t: e.dma_start(out=STG[0:120, t * 128:(t + 1) * 128], in_=stc_rows[t * 120:(t + 1) * 120, c * 128:(c + 1) * 128]), writes=[rSTG])
                    for t in range(4):
                        S.op("pe", lambda e, t=t: e.transpose(PC[:, t * 120:(t + 1) * 120], STG[0:120, t * 128:(t + 1) * 128], IDF[0:120, 0:120]), reads=[rSTG, rCONST], writes=[rPC])
                    S.op("act", lambda e: e.activation(out=STC[:, c, :, :].rearrange("p s r -> p (s r)"), in_=PC[:, 0:480], func=AF.Copy), reads=[rPC], writes=[rSTC])
                return b_stc
            batches.append(mk_stc(0))
            batches.append(mk_stc(1))

            def mk_stf(b):
                def b_stf():
                    stf_rows = stf_d[l].rearrange("s r c -> (s r) c")
                    c0 = b * 8
                    nch = min(8, NFC - c0)
                    S.dma("sp", lambda e: e.dma_start(out=STG[0:32, 0:nch * 128], in_=stf_rows[:, c0 * 128:(c0 + nch) * 128]), writes=[rSTG, rSTGO[0], rSTGO[1]])

                    def trf(e):
                        for j in range(nch):
                            ins = e.transpose(PC[:, j * 32:(j + 1) * 32], STG[0:32, j * 128:(j + 1) * 128], IDF[0:32, 0:32])
                        return ins
                    S.op("pe", trf, reads=[rSTG, rCONST], writes=[rPC])
                    S.op("act", lambda e: e.activation(out=STF[:, c0:c0 + nch, :, :].rearrange("p c s r -> p (c s r)"), in_=PC[:, 0:nch * 32], func=AF.Copy),
                         reads=[rPC], writes=[rSTF])
                return b_stf
            for b in range(6):
                batches.append(mk_stf(b))
            return batches

        prep_layer(0)

        def build_diag(l, c):
            S.op("dve", lambda e: e.tensor_tensor(out=DIAG[:, :, :], in0=IDB[:, :].unsqueeze(1).to_broadcast([128, 31, 128]),
                                                  in1=PARAM[:, l, C_CCW + c:C_CCW + c + 62:2].unsqueeze(2).to_broadcast([128, 31, 128]), op=ALU.mult),
                 reads=[rCONST, rPARAMl[l]], writes=[rDIAG])

        hoisted = set()

        def mixer_norm(l, seg, phase="all"):
            W = SEGW + (NS if seg == 1 else 0)
            xoff = seg * SEGW
            Rap, rR = scr_all[0]
            rRh = [rR, rR2]
            sql = sq_bufs([scr_all[1], scr_all[2], scr_all[3], scr_all[4]])
            if phase in ("sq", "all"):
                rms_stats(seg, W, xoff, sql, Rap, rRh, phase="sq")
            if phase in ("rest", "all"):
                rms_stats(seg, W, xoff, sql, Rap, rRh, phase="rest")
                dq = norm_apply(W, xoff, Rap, rRh, C_NM, l, defer=(phase == "rest"))
                dq.append(lambda: S.transfer([rR, rR2], [rR]))
                dq.append(lambda: build_diag(l, 0))
                if phase == "rest":
                    return dq
                for f_ in dq:
                    f_()
            return []

        def do_layer(l):
            rPARAM = rPARAMl[l]
            sbatches = state_batches(l)
            S.dma("sp", lambda e: e.dma_start(out=sa_o[l, :, 0:1, :], in_=sta_d[l, :, 1:2, :]))
            S.dma("sp", lambda e: e.dma_start(out=sp_o[l, :, 0:14, :], in_=stp_d[l, :, 1:15, :]))
            S.dma("sp", lambda e: e.dma_start(out=sc_o[l, :, 0:29, :], in_=stc_d[l, :, 1:30, :]))
            S.dma("sp", lambda e: e.dma_start(out=sf_o[l, :, 0:1, :], in_=stf_d[l, :, 1:2, :]))
            S.op("dve", lambda e: e.memset(CA_T[:, :, :], 0.0), writes=rCA_Tc)
            S.op("dve", lambda e: e.memset(P_T[:, :, :], 0.0), writes=rP_Tc)
            S.op("dve", lambda e: e.memset(GLU_T[:, :, :], 0.0), writes=rGLU_Tc)
            S.op("dve", lambda e: e.memset(UP_T[:, :, :], 0.0), writes=[rUP_T])

            def do_seg(seg):
                rX = rXs[seg]
                W = SEGW + (NS if seg == 1 else 0)
                xoff = seg * SEGW
                tiles = seg_tiles(seg)
                smp = (seg == 1)
                S.transfer(rACTBc, arena_mix)
                B_ = scr_mix
                Rap, rR = B_[0]
                if (l, seg) not in hoisted:
                    mixer_norm(l, seg)
                first_chunk = [True]
                wv = w_in_d[l].rearrange("(k p) n -> p k n", p=128)

                def load_in(q):
                    return ring_load([(lambda sl: sl[:, :].rearrange("p (k n) -> p k n", k=8), wv[:, :, q * 512:(q + 1) * 512])])

                def zchunk(slot, rslot, j):
                    sv = slot[:, :].rearrange("p (k n) -> p k n", k=8)
                    pt, rpt = next_pslot()
                    if first_chunk[0]:
                        first_chunk[0] = False
                        mm_chunk(pt, rpt, lambda k: sv[:, k, j * 128:(j + 1) * 128], 8, lambda k, c0, n: H[:, k, c0:c0 + n], tiles, [rslot],
                                 tile_reads=[rH0] + [rH1] * (len(tiles) - 1))
                    else:
                        mm_chunk(pt, rpt, lambda k: sv[:, k, j * 128:(j + 1) * 128], 8, lambda k, c0, n: H[:, k, c0:c0 + n], tiles, [rslot] + rHc)
                    return pt, rpt

                sl0, rsl0 = load_in(0)
                sl1, rsl1 = load_in(1)
                AB = [B_[1], B_[2]]
                AC = [B_[3], B_[4]]
                CA = [B_[5], B_[6]]
                for c in range(2):
                    pt, rpt = zchunk(sl0, rsl0, c)
                    S.op("act", lambda e, pt=pt, c=c: e.activation(out=AB[c][0][:, 0:W], in_=pt[:, 0:W], func=AF.Copy), reads=[rpt], writes=[AB[c][1]])
                for c in range(2):
                    pt, rpt = zchunk(sl0, rsl0, 2 + c)
                    S.op("act", lambda e, pt=pt, c=c: e.activation(out=AC[c][0][:, 0:W], in_=pt[:, 0:W], func=AF.Copy), reads=[rpt], writes=[AC[c][1]])
                for c in range(2):
                    pt, rpt = zchunk(sl1, rsl1, c)
                    ca, rca = CA[c]
                    acc, racc = AC[c]
                    S.op("dve", lambda e, c=c, ca=ca: e.tensor_copy(out=ca[:, 0:2], in_=CA_T[:, c, :]), reads=[rCA_Tc[c]], writes=[rca])
                    S.op("dve", lambda e, pt=pt, ca=ca, acc=acc: e.tensor_tensor(out=ca[:, 2:2 + W], in0=pt[:, 0:W], in1=acc[:, 0:W], op=ALU.mult), reads=[rpt, racc], writes=[rca])
                    S.op("dve", lambda e, ca=ca, acc=acc, c=c: e.tensor_scalar(out=acc[:, 0:SEGW], in0=ca[:, 2:2 + SEGW], scalar1=par(l, C_CAW + 4 + c), scalar2=None, op0=ALU.mult),
                         reads=[rca, rPARAM], writes=[racc])
                    S.op("dve", lambda e, ca=ca, acc=acc, c=c: e.scalar_tensor_tensor(out=acc[:, 0:SEGW], in0=ca[:, 1:1 + SEGW], scalar=par(l, C_CAW + 2 + c), in1=acc[:, 0:SEGW],
                                                                                      op0=ALU.mult, op1=ALU.add), reads=[rca, rPARAM, racc], writes=[racc])
                    S.op("dve", lambda e, ca=ca, acc=acc, c=c: e.scalar_tensor_tensor(out=acc[:, 0:SEGW], in0=ca[:, 0:SEGW], scalar=par(l, C_CAW + 0 + c), in1=acc[:, 0:SEGW],
                                                                                      op0=ALU.mult, op1=ALU.add), reads=[rca, rPARAM, racc], writes=[racc])
                    if smp:
                        cs = ca[:, 2 + SEGW:2 + W]
                        S.op("dve", lambda e, acc=acc, cs=cs, c=c: e.tensor_scalar(out=acc[:, SEGW:W], in0=cs, scalar1=par(l, C_CAW + 4 + c), scalar2=None, op0=ALU.mult),
                             reads=[rca, rPARAM], writes=[racc])
                        S.op("dve", lambda e, acc=acc, c=c: e.scalar_tensor_tensor(out=acc[:, SEGW:W], in0=STA[:, c, :, 1], scalar=par(l, C_CAW + 2 + c), in1=acc[:, SEGW:W],
                                                                                  op0=ALU.mult, op1=ALU.add), reads=[rSTA, rPARAM, racc], writes=[racc])
                        S.op("dve", lambda e, acc=acc, c=c: e.scalar_tensor_tensor(out=acc[:, SEGW:W], in0=STA[:, c, :, 0], scalar=par(l, C_CAW + 0 + c), in1=acc[:, SEGW:W],
                                                                                  op0=ALU.mult, op1=ALU.add), reads=[rSTA, rPARAM, racc], writes=[racc])
                        S.op("dve", lambda e, cs=cs, c=c: e.tensor_copy(out=NEWS[:, 0, c, :], in_=cs), reads=[rca], writes=[rNEWS])
                    S.op("dve", lambda e, acc=acc, c=c: e.tensor_tensor(out=Y[:, c, 0:W], in0=acc[:, 0:W], in1=AB[c][0][:, 0:W], op=ALU.mult), reads=[racc, AB[c][1]], writes=[rYc[c]])
                    S.op("dve", lambda e, ca=ca, c=c: e.tensor_copy(out=CA_T[:, c, :], in_=ca[:, SEGW:SEGW + 2]), reads=[rca], writes=[rCA_Tc[c]])

                PBUF = [B_[7], B_[8]]
                TMP = [B_[3], B_[4]]
                for c in range(2):
                    pt, rpt = zchunk(sl1, rsl1, 2 + c)
                    pb, rpb = PBUF[c]
                    S.op("act", lambda e, c=c, pb=pb: e.activation(out=pb[:, 0:15], in_=P_T[:, c, :], func=AF.Copy), reads=[rP_Tc[c]], writes=[rpb])
                    S.op("act", lambda e, pt=pt, pb=pb: e.activation(out=pb[:, 15:15 + W], in_=pt[:, 0:W], func=AF.Copy), reads=[rpt], writes=[rpb])
                    t1, rt1 = TMP[0]
                    t2, rt2 = TMP[1]
                    L = 15 + SEGW
                    S.op("dve", lambda e, pb=pb, t1=t1: e.tensor_tensor(out=t1[:, 1:L], in0=pb[:, 1:L], in1=pb[:, 0:L - 1], op=ALU.add), reads=[rpb], writes=[rt1])
                    S.op("dve", lambda e, t1=t1, t2=t2: e.tensor_tensor(out=t2[:, 3:L], in0=t1[:, 3:L], in1=t1[:, 1:L - 2], op=ALU.add), reads=[rt1], writes=[rt2])
                    if c == 1:
                        S.op("dve", lambda e, t1=t1, t2=t2: e.tensor_tensor(out=t1[:, 7:L], in0=t2[:, 7:L], in1=t2[:, 3:L - 4], op=ALU.add), reads=[rt2, rt1], writes=[rt1])
                        S.op("dve", lambda e, t1=t1, t2=t2: e.tensor_tensor(out=t2[:, 15:L], in0=t1[:, 15:L], in1=t1[:, 7:L - 8], op=ALU.add), reads=[rt1, rt2], writes=[rt2])
                    for hf, (sa, rsa) in enumerate(((t1, rt1), (t2, rt2))):
                        win = WINS[2 * c + hf]
                        pr = slice(hf * 64, (hf + 1) * 64)
                        S.op("dve", lambda e, sa=sa, pb=pb, pr=pr, win=win, c=c: e.scalar_tensor_tensor(out=POOLB[pr, c, 0:SEGW], in0=sa[pr, 15:15 + SEGW], scalar=1.0 / win,
                                                                                                    in1=pb[pr, 15:15 + SEGW], op0=ALU.mult, op1=ALU.subtract),
                             reads=[rsa, rpb], writes=[rPOOLB])
                        if seg == 0:
                            sm = SMALL[pr, 0, 0:15]
                            S.op("dve", lambda e, sa=sa, pr=pr, c=c, sm=sm: e.tensor_tensor(out=sm, in0=sa[pr, 15:30], in1=CORR[pr, c, 0:15], op=ALU.mult), reads=[rsa, rCONST], writes=[rSMALL])
                            S.op("dve", lambda e, pb=pb, pr=pr, c=c, sm=sm: e.tensor_tensor(out=POOLB[pr, c, 0:15], in0=sm, in1=pb[pr, 15:30], op=ALU.subtract), reads=[rSMALL, rpb], writes=[rPOOLB])
                        if smp:
                            sm = SMALL[pr, 1, :]
                            ps_ = pb[pr, 15 + SEGW:15 + W]
                            S.op("dve", lambda e, pr=pr, c=c, win=win, sm=sm: e.tensor_reduce(out=sm, in_=STP[pr, c, :, 16 - win:15], axis=AX.X, op=ALU.add), reads=[rSTP], writes=[rSMALL])
                            S.op("dve", lambda e, sm=sm, ps_=ps_: e.tensor_tensor(out=sm, in0=sm, in1=ps_, op=ALU.add), reads=[rSMALL, rpb], writes=[rSMALL])
                            S.op("dve", lambda e, sm=sm, ps_=ps_, pr=pr, c=c, win=win: e.scalar_tensor_tensor(out=POOLB[pr, c, SEGW:W], in0=sm, scalar=1.0 / win, in1=ps_,
                                                                                                        op0=ALU.mult, op1=ALU.subtract), reads=[rSMALL, rpb], writes=[rPOOLB])
                    if smp:
                        S.op("dve", lambda e, pb=pb, c=c: e.tensor_copy(out=NEWS[:, 1, c, :], in_=pb[:, 15 + SEGW:15 + W]), reads=[rpb], writes=[rNEWS])
                    S.op("dve", lambda e, pb=pb, c=c: e.tensor_copy(out=P_T[:, c, :], in_=pb[:, SEGW:SEGW + 15]), reads=[rpb], writes=[rP_Tc[c]])

                sl2, rsl2 = load_in(2)
                SG = [B_[0], B_[5]]
                t3, rt3 = B_[6]
                for c in range(2):
                    pt, rpt = zchunk(sl2, rsl2, 2 + c)
                    S.op("act", lambda e, pt=pt, c=c: e.activation(out=SG[c][0][:, 0:W], in_=pt[:, 0:W], func=AF.Sigmoid), reads=[rpt], writes=[SG[c][1]])
                sl3, rsl3 = load_in(3)
                U = [B_[1], B_[2]]
                GV = [B_[3], B_[4]]
                for c in range(2):
                    pt, rpt = zchunk(sl3, rsl3, c)
                    S.op("act", lambda e, pt=pt, c=c: e.activation(out=U[c][0][:, 0:W], in_=pt[:, 0:W], func=AF.Gelu_apprx_tanh), reads=[rpt], writes=[U[c][1]])
                tmpb4 = ARENA[:, ascr_offs[2]:ascr_offs[2] + 4 * SCRW].rearrange("p (c t) -> p c t", c=4)
                rtmpb = [rASCR[2], rASCR[3]]
                for c in range(2):
                    pt, rpt = zchunk(sl3, rsl3, 2 + c)
                    S.op("act", lambda e, pt=pt, c=c: e.activation(out=GV[c][0][:, 0:W], in_=pt[:, 0:W], func=AF.Gelu_apprx_tanh), reads=[rpt], writes=[GV[c][1]])
                    ln_prep(GV[c], c, W, tmpb4, rtmpb)
                sqrt_preload()

                csamp = []
                for c in range(2):
                    pt, rpt = zchunk(sl2, rsl2, c)
                    sg, rsg = SG[c]
                    S.op("dve", lambda e, pt=pt, sg=sg: e.tensor_tensor(out=sg[:, 0:W], in0=pt[:, 0:W], in1=sg[:, 0:W], op=ALU.mult), reads=[rpt, rsg], writes=[rsg])
                    S.op("act", lambda e, c=c: e.activation(out=GLUB[:, c, 0:30], in_=GLU_T[:, c, :], func=AF.Copy), reads=[rGLU_Tc[c]], writes=[rGLUBc[c]])
                    S.op("act", lambda e, sg=sg, c=c: e.activation(out=GLUB[:, c, 30:30 + W], in_=sg[:, 0:W], func=AF.Copy), reads=[rsg], writes=[rGLUBc[c]])
                    S.op("dve", lambda e, sg=sg, c=c: e.tensor_copy(out=GLU_T[:, c, :], in_=sg[:, SEGW - 30:SEGW]), reads=[rsg], writes=[rGLU_Tc[c]])
                    if smp:
                        gs = sg[:, SEGW:W]
                        S.op("dve", lambda e, gs=gs, c=c: e.tensor_copy(out=NEWS[:, 2, c, :], in_=gs), reads=[rsg], writes=[rNEWS])
                        t3v = t3[:, 0:NS * 30].rearrange("p (s k) -> p s k", k=30)
                        wbc = PARAM[:, l, C_CCW + c:C_CCW + c + 60:2].unsqueeze(1).to_broadcast([128, NS, 30])
                        S.op("dve", lambda e, t3v=t3v, wbc=wbc, c=c: e.tensor_tensor(out=t3v, in0=STC[:, c, :, :], in1=wbc, op=ALU.mult), reads=[rSTC, rPARAM], writes=[rt3])
                        sm = SMALL[:, 2 + c, :]
                        S.op("dve", lambda e, t3v=t3v, sm=sm: e.tensor_reduce(out=sm, in_=t3v, axis=AX.X, op=ALU.add), reads=[rt3], writes=[rSMALL])
                        S.op("dve", lambda e, gs=gs, sm=sm, c=c: e.scalar_tensor_tensor(out=sm, in0=gs, scalar=par(l, C_CCW + 60 + c), in1=sm, op0=ALU.mult, op1=ALU.add),
                             reads=[rsg, rSMALL, rPARAM], writes=[rSMALL])

                VT = ARENA[:, ascr_offs[3]:ascr_offs[3] + 8 * 256].rearrange("p (i c) -> p i c", i=8)
                rVT = rASCR[3]
                mean_ap, rmean = B_[0]
                rstd_ap, rrstd = B_[5]
                ln_stats(GV, W, seg, tmpb4, rtmpb, mean_ap, rmean, rstd_ap, rrstd, prep_done=True)

                for c in range(2):
                    pt2, rpt2 = next_pslot()
                    mm_chunk(pt2, rpt2, lambda k, c=c: WG[:, c, :], 1, lambda k, c0, n, c=c: POOLB[:, c, c0:c0 + n], tiles, [rWG, rPOOLB])
                    S.op("act", lambda e, pt2=pt2, c=c: e.activation(out=Y[:, 2 + c, 0:W], in_=pt2[:, 0:W], func=AF.Copy, scale=par(l, C_PSC + c)), reads=[rpt2, rPARAM], writes=[rYc[2 + c]])

                for c in range(2):
                    gv, rgv = GV[c]
                    S.op("dve", lambda e, gv=gv: e.tensor_tensor(out=gv[:, 0:W], in0=gv[:, 0:W], in1=mean_ap[:, 0:W], op=ALU.subtract), reads=[rgv, rmean], writes=[rgv])
                    S.op("dve", lambda e, gv=gv: e.tensor_tensor(out=gv[:, 0:W], in0=gv[:, 0:W], in1=rstd_ap[:, 0:W], op=ALU.mult), reads=[rgv, rrstd], writes=[rgv])
                    S.op("dve", lambda e, gv=gv, c=c: e.tensor_scalar(out=gv[:, 0:W], in0=gv[:, 0:W], scalar1=par(l, C_LDG + c), scalar2=par(l, C_LDB + c), op0=ALU.mult, op1=ALU.add),
                         reads=[rgv, rPARAM], writes=[rgv])
                    S.op("act", lambda e, gv=gv, c=c: e.activation(out=VB[:, c, 0:SEGW], in_=gv[:, 0:SEGW], func=AF.Copy), reads=[rgv], writes=[rVB])
                    if smp:
                        S.op("dve", lambda e, gv=gv, c=c: e.tensor_copy(out=NEWS[:, 3, c, :], in_=gv[:, SEGW:W]), reads=[rgv], writes=[rNEWS])

                XC = [B_[6], B_[0]]

                def conv(c):
                    pt2, rpt2 = next_pslot()

                    def cv(e):
                        ins = None
                        for (c0, n) in [(0, 512), (512, 512)]:
                            for k in range(31):
                                ins = e.matmul(pt2[:, c0:c0 + n], lhsT=DIAG[:, k, :], rhs=GLUB[:, c, c0 + k:c0 + k + n], start=(k == 0), stop=(k == 30))
                        return ins
                    S.op("pe", cv, reads=[rDIAG, rGLUBc[c]], writes=[rpt2])
                    xc, rxc = XC[c]
                    S.op("act", lambda e: e.activation(out=xc[:, 0:SEGW], in_=pt2[:, 0:SEGW], func=AF.Identity, bias=par(l, C_CCB + c), scale=1.0),
                         reads=[rpt2, rPARAM], writes=[rxc])
                    if smp:
                        sm = SMALL[:, 2 + c, :]
                        S.op("dve", lambda e: e.tensor_scalar(out=xc[:, SEGW:W], in0=sm, scalar1=par(l, C_CCB + c), scalar2=None, op0=ALU.add),
                             reads=[rSMALL, rPARAM], writes=[rxc])
                conv(0)

                PCb = PC[:, :].bitcast(BF16)
                PDb = PD[:, :].bitcast(BF16)
                for half in range(2):
                    pcb, rpcb = (PCb, rPC) if half == 0 else (PDb, rPD)

                    def trv(e, half=half, pcb=pcb):
                        ins = None
                        for i in range(4):
                            for c in range(2):
                                tbk = half * 4 + i
                                ins = e.transpose(pcb[:, i * 256 + c * 128:i * 256 + (c + 1) * 128], VB[:, c, tbk * 128:(tbk + 1) * 128], IDB[:, :])
                        return ins
                    S.op("pe", trv, reads=[rVB, rCONST], writes=[rpcb])
                    S.op("act", lambda e, half=half, pcb=pcb: e.activation(out=VT[:, half * 4:(half + 1) * 4, :], in_=pcb[:, 0:1024].rearrange("p (i c) -> p i c", i=4), func=AF.Copy),
                         reads=[rpcb], writes=[rVT])
                build_diag(l, 1)
                for c in range(2):
                    pt, rpt = next_pslot()

                    def gate(e, c=c, pt=pt):
                        ins = None
                        for i in range(8):
                            for hf in range(2):
                                hd = 2 * c + hf
                                ins = e.matmul(pt[hf * 64:(hf + 1) * 64, i * 128:(i + 1) * 128], lhsT=VT[:, i, hd * 64:(hd + 1) * 64], rhs=WST[:, hd, :], start=True, stop=True)
                        return ins
                    S.op("pe", gate, reads=[rVT, rWST], writes=[rpt])
                    gv, rgv = GV[c]
                    u, ru = U[c]
                    bsb_bc = BSB[:, c, :].unsqueeze(1).to_broadcast([128, 8, 128])
                    S.op("dve", lambda e, pt=pt, gv=gv, bsb_bc=bsb_bc: e.tensor_tensor(out=gv[:, 0:SEGW].rearrange("p (i t) -> p i t", i=8), in0=pt[:, 0:SEGW].rearrange("p (i t) -> p i t", i=8),
                                                                                    in1=bsb_bc, op=ALU.add), reads=[rpt, rBSB, rgv], writes=[rgv])
                    if smp:
                        S.op("dve", lambda e, gv=gv, c=c: e.tensor_scalar(out=gv[:, SEGW:W], in0=gv[:, SEGW:W], scalar1=WS00[:, c:c + 1], scalar2=BSB[:, c, 0:1], op0=ALU.mult, op1=ALU.add),
                             reads=[rgv, rWS00, rBSB], writes=[rgv])
                    S.op("dve", lambda e, gv=gv, u=u, c=c: e.tensor_tensor(out=Y[:, 6 + c, 0:W], in0=gv[:, 0:W], in1=u[:, 0:W], op=ALU.mult), reads=[rgv, ru], writes=[rYc[6 + c]])
                rcpC = [Res("LNCcp0"), Res("LNCcp1")]
                ln_prep(XC[0], 0, W, tmpb4, rtmpb, rcp=rcpC)
                conv(1)
                ln_prep(XC[1], 1, W, tmpb4, rtmpb, rcp=rcpC)
                sqrt_preload()

                meanc_ap, rmeanc = B_[5]
                rstdc_ap, rrstdc = B_[1]
                ln_stats(XC, W, seg, tmpb4, rtmpb, meanc_ap, rmeanc, rstdc_ap, rrstdc, prep_done=True, do_recip=False, rcp=rcpC)
                for h, (a, b) in enumerate(((0, 512), (512, W))):
                    S.op("dve", lambda e, a=a, b=b: e.reciprocal(out=rstdc_ap[:, a:b], in_=rstdc_ap[:, a:b]), reads=[rrstdc], writes=[rrstdc])
                    for c in range(2):
                        xc, rxc = XC[c]
                        S.op("dve", lambda e, xc=xc, a=a, b=b: e.tensor_tensor(out=xc[:, a:b], in0=xc[:, a:b], in1=meanc_ap[:, a:b], op=ALU.subtract), reads=[rxc, rmeanc], writes=[rxc])
                        S.op("dve", lambda e, xc=xc, a=a, b=b: e.tensor_tensor(out=xc[:, a:b], in0=xc[:, a:b], in1=rstdc_ap[:, a:b], op=ALU.mult), reads=[rxc, rrstdc], writes=[rxc])
                        S.op("act", lambda e, xc=xc, c=c, a=a, b=b: e.activation(out=Y[:, 4 + c, a:b], in_=xc[:, a:b], func=AF.Silu, bias=par(l, C_LCB + c), scale=par(l, C_LCG + c)),
                             reads=[rxc, rPARAM], writes=[rY45[c][h]])

                wo = w_out_d[l].rearrange("(k p) n -> p k n", p=128)
                slo = []
                for q in range(2):
                    sl_, rsl_ = ring_load([(lambda sl: sl[:, :].rearrange("p (k n) -> p k n", k=8), wo[:, :, q * 512:(q + 1) * 512])])
                    slo.append((sl_[:, :].rearrange("p (k n) -> p k n", k=8), rsl_))
                Rap, rR = scr_all[0]
                rRh = [rR, rR2]
                fbufs = [B_[2], B_[3], B_[4], B_[7]]
                fsq = sq_bufs(fbufs)
                fbres = [b[1] for b in fbufs]
                rFSQ = [[Res("FSQ%d_%d" % (c, h)) for h in range(2)] for c in range(8)]
                colr = [(0, 512), (512, W)]
                htiles = [[(0, 512)], [t for t in tiles if t[0] >= 512]]
                sqrt_preload()

                def oproj_part(pt, rpt, sv, j, ks, first, last, reads, tl):
                    def f(e):
                        ins = None
                        for (c0, n) in tl:
                            for i, k in enumerate(ks):
                                ins = e.matmul(pt[:, c0:c0 + n], lhsT=sv[:, k, j * 128:(j + 1) * 128], rhs=Y[:, k, c0:c0 + n],
                                               start=(first and i == 0), stop=(last and i == len(ks) - 1))
                        return ins
                    S.op("pe", f, reads=reads, writes=[rpt])

                def evac(pt, rpt, c, h):
                    a, b = colr[h]
                    S.op("dve", lambda e: e.tensor_tensor(out=X[:, c, xoff + a:xoff + b], in0=pt[:, a:b], in1=X[:, c, xoff + a:xoff + b], op=ALU.add),
                         reads=[rpt, rX[c]], writes=[rX[c], rXhs[seg][c][h]])
                    S.op("act", lambda e: e.activation(out=fsq[c][0][:, a:b], in_=X[:, c, xoff + a:xoff + b], func=AF.Square),
                         reads=[rXhs[seg][c][h]], writes=[rFSQ[c][h]])

                def ffn_norm_rest(h):
                    a, b = colr[h]
                    tl = htiles[h]
                    pt, rpt = next_pslot()

                    def f1(e):
                        ins = None
                        for (c0, n) in tl:
                            for k in range(7):
                                ins = e.matmul(pt[:, c0:c0 + n], lhsT=ONESB[:, :], rhs=fsq[k][0][:, c0:c0 + n], start=(k == 0), stop=False)
                        return ins

                    def f2(e):
                        ins = None
                        for (c0, n) in tl:
                            ins = e.matmul(pt[:, c0:c0 + n], lhsT=ONESB[:, :], rhs=fsq[7][0][:, c0:c0 + n], start=False, stop=True)
                        return ins
                    S.op("pe", f1, reads=[rFSQ[k][h] for k in range(7)] + [rCONST], writes=[rpt])
                    S.op("pe", f2, reads=[rFSQ[7][h], rCONST] + (fbres if h == 1 else []), writes=[rpt])
                    S.op("act", lambda e: e.activation(out=Rap[:, a:b], in_=pt[:, a:b], func=AF.Sqrt, bias=EPS_T[:, 0:1], scale=1.0 / D), reads=[rpt, rCONST], writes=[rRh[h]])
                    dq = []
                    dq.append(lambda: S.op("dve", lambda e: e.reciprocal(out=Rap[:, a:b], in_=Rap[:, a:b]), reads=[rRh[h]], writes=[rRh[h]]))
                    for c in range(8):
                        dq.append(lambda c=c: S.op("dve", lambda e: e.scalar_tensor_tensor(out=H[:, c, a:b], in0=X[:, c, xoff + a:xoff + b], scalar=par(l, C_NF + c), in1=Rap[:, a:b],
                                                                                         op0=ALU.mult, op1=ALU.mult), reads=[rXhs[seg][c][h], rRh[h], rPARAM], writes=[rHch[c][h]]))
                    return dq

                S.transfer(fbres, [r for pair in rFSQ for r in pair] + fbres)
                early = [0, 1, 2, 3, 6, 7]
                rY_early = [rYc[k] for k in early]

                def evac_full(pt, rpt, c):
                    S.op("dve", lambda e: e.tensor_tensor(out=X[:, c, xoff:xoff + W], in0=pt[:, 0:W], in1=X[:, c, xoff:xoff + W], op=ALU.add),
                         reads=[rpt, rX[c]], writes=[rX[c], rXhs[seg][c][0], rXhs[seg][c][1]])
                    S.op("act", lambda e: e.activation(out=fsq[c][0][:, 0:W], in_=X[:, c, xoff:xoff + W], func=AF.Square),
                         reads=[rXhs[seg][c][0], rXhs[seg][c][1]], writes=[rFSQ[c][0], rFSQ[c][1]])
                sv, rslo = slo[0]
                p0 = next_pslot()
                p1 = next_pslot()
                oproj_part(p0[0], p0[1], sv, 0, early, True, False, [rslo] + rY_early, tiles)
                oproj_part(p1[0], p1[1], sv, 1, early, True, False, [rslo] + rY_early, tiles)
                oproj_part(p0[0], p0[1], sv, 0, [4, 5], False, True, [rslo, rY45[0][0], rY45[1][0]], htiles[0])
                oproj_part(p1[0], p1[1], sv, 1, [4, 5], False, True, [rslo, rY45[0][0], rY45[1][0]], htiles[0])
                dq0 = []
                for c in range(2, 8):
                    sv, rslo = slo[c // 4]
                    j = c % 4
                    pt, rpt = (PC, rPC) if c % 2 == 0 else (PD, rPD)
                    mm_chunk(pt, rpt, lambda k, sv=sv, j=j: sv[:, k, j * 128:(j + 1) * 128], 8, lambda k, c0, n: Y[:, k, c0:c0 + n], htiles[0],
                             [rslo] + rY_early + [rY45[0][0], rY45[1][0]])
                    evac(pt, rpt, c, 0)
                sv, rslo = slo[0]
                oproj_part(p0[0], p0[1], sv, 0, [4, 5], False, True, [rslo, rY45[0][1], rY45[1][1]], htiles[1])
                evac_full(p0[0], p0[1], 0)
                oproj_part(p1[0], p1[1], sv, 1, [4, 5], False, True, [rslo, rY45[0][1], rY45[1][1]], htiles[1])
                evac_full(p1[0], p1[1], 1)
                for c in range(2, 8):
                    sv, rslo = slo[c // 4]
                    j = c % 4
                    pt, rpt = next_pslot()
                    mm_chunk(pt, rpt, lambda k, sv=sv, j=j: sv[:, k, j * 128:(j + 1) * 128], 8, lambda k, c0, n: Y[:, k, c0:c0 + n], htiles[1],
                             [rslo] + rY_early + [rY45[0][1], rY45[1][1]])
                    evac(pt, rpt, c, 1)
                    if c == 2:
                        dq0 = ffn_norm_rest(0)
                    if c >= 3:
                        for _ in range(3):
                            if dq0:
                                dq0.pop(0)()
                while dq0:
                    dq0.pop(0)()
                for f_ in ffn_norm_rest(1):
                    f_()
                S.transfer([rR, rR2], [rR])

                obatches = []
                if smp:
                    for c in range(2):
                        cs = slice(c * 128, (c + 1) * 128)
                        obatches.append(lambda c=c, cs=cs: tr_out(CA_T[:, c, :], 2, pa_o[l, :, cs], [rCA_Tc[c]]))
                        obatches.append(lambda c=c, cs=cs: tr_out(P_T[:, c, :], 15, pp_o[l, :, cs], [rP_Tc[c]]))
                        obatches.append(lambda c=c, cs=cs: tr_out(GLU_T[:, c, :], 30, pc_o[l, :, cs], [rGLU_Tc[c]]))
                        obatches.append(lambda c=c, cs=cs: tr_out(NEWS[:, 0, c, :], NS, sa_o[l, :, 1, cs], [rNEWS]))
                        obatches.append(lambda c=c, cs=cs: tr_out(NEWS[:, 1, c, :], NS, sp_o[l, :, 14, cs], [rNEWS]))
                        obatches.append(lambda c=c, cs=cs: tr_out(NEWS[:, 2, c, :], NS, sc_o[l, :, 29, cs], [rNEWS]))
                        obatches.append(lambda c=c, cs=cs: tr_out(NEWS[:, 3, c, :], NS, sv_o[l, :, cs], [rNEWS]))
                S.transfer(arena_mix, rACTBc)
                first_up = [True]
                wu = w_up_d[l].rearrange("(k p) n -> p k n", p=128)
                UG, UA, TG0, TA = scr_all[1], scr_all[2], scr_all[3], scr_all[4]
                DIAGF = DIAG[:, :, :].rearrange("p k n -> p (k n)")[:, 0:2 * SCRW].bitcast(F32)
                TGs = [TG0, (DIAGF, rDIAG)]

                def ups_out(qq):
                    g0, a0 = 2 * qq, 22 + 2 * qq
                    tr_out_multi([UPS[:, g0, :], UPS[:, g0 + 1, :], UPS[:, a0, :], UPS[:, a0 + 1, :]], NS,
                                 [(sf_o[l, :, 1, g0 * 128:(g0 + 2) * 128], 0, 256), (sf_o[l, :, 1, a0 * 128:(a0 + 2) * 128], 256, 256)],
                                 [rUPSc[g0], rUPSc[g0 + 1], rUPSc[a0], rUPSc[a0 + 1]])
                for q in range(11):
                    slu, rslu = ring_load([
                        (lambda sl: sl[:, :].rearrange("p (k n) -> p k n", k=8)[:, :, 0:256], wu[:, :, q * 256:(q + 1) * 256]),
                        (lambda sl: sl[:, :].rearrange("p (k n) -> p k n", k=8)[:, :, 256:512], wu[:, :, DFF + q * 256:DFF + (q + 1) * 256]),
                    ])
                    sv = slu[:, :].rearrange("p (k n) -> p k n", k=8)
                    for jj in range(2):
                        j = q * 2 + jj
                        outs = []
                        TG = TGs[j % 2]
                        pre = None
                        if first_up[0]:
                            first_up[0] = False
                            pre = [next_pslot(), next_pslot()]
                            for wh in range(2):
                                mm_chunk(pre[wh][0], pre[wh][1], lambda k, sv=sv, jj=jj, wh=wh: sv[:, k, wh * 256 + jj * 128:wh * 256 + (jj + 1) * 128], 8,
                                         lambda k, c0, n: H[:, k, c0:c0 + n], [tiles[0]], [rslu] + rH0)
                        for which, (ub, rub), (tbuf, rtb_) in ((0, UG, TG), (1, UA, TA)):
                            cc = j + which * 22
                            if pre is not None:
                                pt, rpt = pre[which]
                                mm_chunk(pt, rpt, lambda k, sv=sv, jj=jj, which=which: sv[:, k, which * 256 + jj * 128:which * 256 + (jj + 1) * 128], 8,
                                         lambda k, c0, n: H[:, k, c0:c0 + n], tiles[1:], [rslu] + rH1)
                            else:
                                pt, rpt = next_pslot()
                                mm_chunk(pt, rpt, lambda k, sv=sv, jj=jj, which=which: sv[:, k, which * 256 + jj * 128:which * 256 + (jj + 1) * 128], 8,
                                         lambda k, c0, n: H[:, k, c0:c0 + n], tiles, [rslu] + rHc)
                            w0, w1, w2 = (par(l, C_CFW + kk * NFC + cc) for kk in range(3))
                            S.op("act", lambda e, ub=ub, cc=cc: e.activation(out=ub[:, 0:2], in_=UP_T[:, :, cc], func=AF.Copy), reads=[rUP_T], writes=[rub])
                            S.op("act", lambda e, ub=ub, pt=pt: e.activation(out=ub[:, 2:2 + W], in_=pt[:, 0:W], func=AF.Copy), reads=[rpt], writes=[rub])
                            S.op("act", lambda e, tbuf=tbuf, pt=pt, w2=w2: e.activation(out=tbuf[:, 0:W], in_=pt[:, 0:W], func=AF.Copy, scale=w2), reads=[rpt, rPARAM], writes=[rtb_])
                            S.op("act", lambda e, pt=pt, cc=cc: e.activation(out=UP_T[:, :, cc], in_=pt[:, SEGW - 2:SEGW], func=AF.Copy), reads=[rpt], writes=[rUP_T])
                            S.op("dve", lambda e, ub=ub, tbuf=tbuf, w1=w1: e.scalar_tensor_tensor(out=tbuf[:, 0:SEGW], in0=ub[:, 1:1 + SEGW], scalar=w1, in1=tbuf[:, 0:SEGW], op0=ALU.mult, op1=ALU.add),
                                 reads=[rub, rPARAM, rtb_], writes=[rtb_])
                            S.op("dve", lambda e, ub=ub, tbuf=tbuf, w0=w0: e.scalar_tensor_tensor(out=tbuf[:, 0:SEGW], in0=ub[:, 0:SEGW], scalar=w0, in1=tbuf[:, 0:SEGW], op0=ALU.mult, op1=ALU.add),
                                 reads=[rub, rPARAM, rtb_], writes=[rtb_])
                            if smp:
                                us = ub[:, 2 + SEGW:2 + W]
                                S.op("dve", lambda e, tbuf=tbuf, w1=w1, cc=cc: e.scalar_tensor_tensor(out=tbuf[:, SEGW:W], in0=STF[:, cc, :, 1], scalar=w1, in1=tbuf[:, SEGW:W], op0=ALU.mult, op1=ALU.add),
                                     reads=[rSTF, rPARAM, rtb_], writes=[rtb_])
                                S.op("dve", lambda e, tbuf=tbuf, w0=w0, cc=cc: e.scalar_tensor_tensor(out=tbuf[:, SEGW:W], in0=STF[:, cc, :, 0], scalar=w0, in1=tbuf[:, SEGW:W], op0=ALU.mult, op1=ALU.add),
                                     reads=[rSTF, rPARAM, rtb_], writes=[rtb_])
                                S.op("dve", lambda e, us=us, cc=cc: e.tensor_copy(out=UPS[:, cc, :], in_=us), reads=[rub], writes=[rUPSc[cc]])
                        tg, rtg = TG
                        ta, rta = TA
                        S.op("act", lambda e, tg=tg: e.activation(out=tg[:, 0:W], in_=tg[:, 0:W], func=AF.Silu), reads=[rtg], writes=[rtg])
                        S.op("dve", lambda e, tg=tg, ta=ta, j=j: e.tensor_tensor(out=ACTB[:, j, 0:W], in0=tg[:, 0:W], in1=ta[:, 0:W], op=ALU.mult), reads=[rtg, rta], writes=[rACTBc[j]])
                    if seg == 0 and q < len(sbatches):
                        sbatches[q]()
                    if smp:
                        if q >= 1:
                            ups_out(q - 1)
                        if obatches and q >= 1:
                            obatches.pop(0)()
                    if seg == 1 and q == 1 and l + 1 < DEPTH:
                        prep_layer_dma(l + 1, with_wsf=False)
                    if seg == 0 and q == 10 and l + 1 < DEPTH:
                        load_params_dma(l + 1)
                        wsf_dma_hi(l + 1)
                    if seg == 1 and q == 0 and l + 1 < DEPTH:
                        prep_layer_compute(l + 1, hi=True)
                        load_params_compute(l + 1, ps=(PC, rPC))
                nxt = (l, 1) if seg == 0 else ((l + 1, 0) if l + 1 < DEPTH else None)
                if nxt is not None:
                    mixer_norm(nxt[0], nxt[1], phase="sq")
                    hoisted.add(nxt)
                wd = w_down_d[l].rearrange("(k p) n -> p k n", p=128)
                hq = []
                for c in range(8):
                    if c == 2 and nxt is not None:
                        hq = mixer_norm(nxt[0], nxt[1], phase="rest")
                    sld, rsld = ring_load([(lambda sl: sl[:, 0:22 * 128].rearrange("p (k n) -> p k n", k=22), wd[:, :, c * 128:(c + 1) * 128])])
                    sv = sld[:, 0:22 * 128].rearrange("p (k n) -> p k n", k=22)
                    pt, rpt = next_pslot()
                    if c == 0:
                        def dpart(ks, first, last, reads, pt=pt, rpt=rpt, sv=sv):
                            def f(e):
                                ins = None
                                for (c0, n) in tiles:
                                    for i, k in enumerate(ks):
                                        ins = e.matmul(pt[:, c0:c0 + n], lhsT=sv[:, k, :], rhs=ACTB[:, k, c0:c0 + n], start=(first and i == 0), stop=(last and i == len(ks) - 1))
                                return ins
                            S.op("pe", f, reads=reads, writes=[rpt])
                        dpart(list(range(20)), True, False, [rsld] + rACTBc[0:20])
                        dpart([20, 21], False, True, [rsld] + rACTBc[20:22])
                    else:
                        mm_chunk(pt, rpt, lambda k, sv=sv: sv[:, k, :], 22, lambda k, c0, n: ACTB[:, k, c0:c0 + n], tiles, [rsld] + rACTBc)
                    S.op("dve", lambda e, pt=pt, c=c: e.tensor_tensor(out=X[:, c, xoff:xoff + W], in0=pt[:, 0:W], in1=X[:, c, xoff:xoff + W], op=ALU.add), reads=[rpt, rX[c]], writes=[rX[c], rXhs[seg][c][0], rXhs[seg][c][1]])
                    for _ in range(5):
                        if hq:
                            hq.pop(0)()
                    if smp and c == 1:
                        ups_out(10)
                    if smp and c >= 2:
                        if obatches:
                            obatches.pop(0)()
                while hq:
                    hq.pop(0)()
                assert not obatches

            for seg_ in range(2):
                do_seg(seg_)
            for r in range(2):
                tr_out(UP_T[:, r, :], NFC, pf_o[l, r, :].rearrange("(c p) -> c p", p=128), [rUP_T], eng="dve")

        for l_ in range(DEPTH):
            do_layer(l_)

        rXNc = [Res("XN%d" % c) for c in range(8)]

        def fin_seg(seg):
            rX = rXs[seg]
            W = SEGW + (NS if seg == 1 else 0)
            xoff = seg * SEGW
            S.transfer(rACTBc + arena_mix + rXNc, rXNc)
            Rap, rR = scr_all[0]
            rms_stats(seg, W, xoff, sq_bufs([scr_all[1], scr_all[2], scr_all[3], scr_all[4]]), Rap, [rR, rR2])
            S.op("dve", lambda e: e.reciprocal(out=Rap[:, 0:W], in_=Rap[:, 0:W]), reads=[rR, rR2], writes=[rR, rR2])
            S.transfer([rR, rR2], [rR])
            XN = ARENA[:, 0:2 * 8 * 1040].bitcast(F32).rearrange("p (c t) -> p c t", c=8)
            for c in range(8):
                S.op("dve", lambda e, c=c: e.scalar_tensor_tensor(out=XN[:, c, 0:W], in0=X[:, c, xoff:xoff + W], scalar=NFIN[:, c:c + 1],
                                                                  in1=Rap[:, 0:W], op0=ALU.mult, op1=ALU.mult), reads=[rX[c], rR, rNFIN], writes=[rXNc[c]])
            nblk = 8 + (1 if seg == 1 else 0)
            for tb in range(nblk):
                ntok = 128 if tb < 8 else NS
                c0 = tb * 128
                pt, rpt = next_pslot()

                def trx(e, pt=pt, ntok=ntok, c0=c0):
                    ins = None
                    for c in range(8):
                        ins = e.transpose(pt[0:ntok, c * 128:(c + 1) * 128], XN[:, c, c0:c0 + ntok], IDF[:, :])
                    return ins
                S.op("pe", trx, reads=rXNc + [rCONST], writes=[rpt])
                ob, rob = scr_all[1 + tb % 4]
                if tb % 2 == 0:
                    S.op("act", lambda e, ob=ob, pt=pt, ntok=ntok: e.activation(out=ob[0:ntok, 0:1024], in_=pt[0:ntok, 0:1024], func=AF.Copy), reads=[rpt], writes=[rob])
                else:
                    S.op("dve", lambda e, ob=ob, pt=pt, ntok=ntok: e.tensor_copy(out=ob[0:ntok, 0:1024], in_=pt[0:ntok, 0:1024]), reads=[rpt], writes=[rob])
                if tb < 8:
                    dst = yp_o[xoff + c0:xoff + c0 + 128, :]
                else:
                    dst = ys_o[:, :]
                S.dma("sp", lambda e, ob=ob, dst=dst, ntok=ntok: e.dma_start(out=dst, in_=ob[0:ntok, 0:1024]), reads=[rob])

        for seg_ in range(2):
            fin_seg(seg_)
        S.finish()

        with nc.Block() as block:
            @block.tensor
            def _(e):
                for f in S.lists["pe"]:
                    f(e)

            @block.scalar
            def _(e):
                for f in S.lists["act"]:
                    f(e)

            @block.vector
            def _(e):
                for f in S.lists["dve"]:
                    f(e)

            @block.gpsimd
            def _(e):
                for f in S.lists["pool"]:
                    f(e)

            @block.sync
            def _(e):
                for f in S.lists["sp"]:
                    f(e)
    return nc


_NC_CACHE = {}


def kernel(**inputs):
    inp = {k: np.ascontiguousarray(np.asarray(v)) for k, v in inputs.items()}
    if "nc" not in _NC_CACHE:
        _NC_CACHE["nc"] = build_nc()
    nc = _NC_CACHE["nc"]
    shared = ["norm_mix", "w_in", "conv_a_w", "pool_w", "pool_scale", "conv_c_w", "conv_c_b", "ln_c_g", "ln_c_b", "ln_d_g", "ln_d_b",
              "w_s", "b_s", "w_out", "norm_ffn", "w_up", "conv_f_w", "w_down", "norm_final"]
    in_maps = []
    for i in range(NCORES):
        m = {k: inp[k] for k in shared}
        m["x_prompt"] = np.ascontiguousarray(inp["x_prompt"][i])
        m["x_sample"] = np.ascontiguousarray(inp["x_sample"][i * NS:(i + 1) * NS, 0, :])
        for k in ("state_conv_a", "state_pool", "state_conv_c", "state_conv_ffn"):
            m[k] = np.ascontiguousarray(inp[k][:, i * NS:(i + 1) * NS])
        in_maps.append(m)
    res = run_bass_kernel_spmd(nc, in_maps, core_ids=list(range(NCORES)))
    R = res.results
    y_prompt = np.stack([R[i]["y_prompt"] for i in range(NCORES)], axis=0)
    y_sample = np.concatenate([R[i]["y_sample"] for i in range(NCORES)], axis=0)[:, None, :]

    def pstack(name):
        return np.stack([R[i][name] for i in range(NCORES)], axis=1)

    def scat(name):
        return np.concatenate([R[i][name] for i in range(NCORES)], axis=1)
    outs = (y_prompt, y_sample,
            pstack("new_conv_a_prompt"), pstack("new_pool_prompt"), pstack("new_conv_c_prompt"), pstack("new_conv_ffn_prompt"),
            scat("new_conv_a_sample"), scat("new_pool_sample"), scat("new_conv_c_sample"), scat("new_conv_ffn_sample"),
            scat("new_chunk_v_sample")[:, :, None, :])
    return tuple(np.ascontiguousarray(o.astype(np.float32)) for o in outs)
```

```python
import numpy as np
from contextlib import ExitStack
import concourse.bass as bass
import concourse.mybir as mybir
from concourse.bass_utils import run_bass_kernel_spmd

F32 = mybir.dt.float32
BF16 = mybir.dt.bfloat16
AF = mybir.ActivationFunctionType
ALU = mybir.AluOpType
AX = mybir.AxisListType

NCORES = 8
D = 1024
SEQ = 2048
DEPTH = 4
NS = 16
DFF = 2816
NFC = 44
SEGW = 1024
TT = SEQ + NS
EPS = 1e-6
WINS = (2, 4, 8, 16)

C_NM, C_NF, C_CAW, C_PSC, C_CCW, C_CCB, C_LCG, C_LCB, C_LDG, C_LDB, C_CFW = 0, 8, 16, 22, 24, 86, 88, 90, 92, 94, 96
NPAR = 228


class Res:
    __slots__ = ("name", "w", "rs")

    def __init__(self, name):
        self.name = name
        self.w = None
        self.rs = {}


class Sched:
    K = 16

    def __init__(self, nc, stack):
        self.nc = nc
        self.lists = {e: [] for e in ("pe", "act", "dve", "pool", "sp")}
        self.sem = {e: stack.enter_context(nc.semaphore("c_" + e)) for e in ("pe", "act", "dve", "pool")}
        self.cnt = {e: 0 for e in self.sem}
        self.seen = {e: {} for e in self.lists}
        self.dq = {q: [stack.enter_context(nc.semaphore("d_%s%d" % (q, i))) for i in range(self.K)] for q in ("sp", "pool", "act")}
        self.dcnt = {"sp": 0, "pool": 0, "act": 0}
        self.semobj = {}

    def _deps(self, reads, writes):
        deps = []
        for r in reads:
            if r.w is not None:
                deps.append(r.w)
        for w in writes:
            if w.w is not None:
                deps.append(w.w)
            deps.extend(w.rs.values())
        return deps

    def _waits(self, eng, deps):
        for (sem, val, src) in deps:
            if src == "pe" and eng == "pe":
                continue
            key = id(sem)
            if self.seen[eng].get(key, 0) >= val:
                continue
            self.seen[eng][key] = val
            self.lists[eng].append(lambda e, sem=sem, val=val: e.wait_ge(sem, val))

    def _update(self, tok, reads, writes):
        for r in reads:
            key = id(tok[0])
            old = r.rs.get(key)
            if old is None or old[1] < tok[1]:
                r.rs[key] = tok
        for w in writes:
            w.w = tok
            w.rs = {}

    def op(self, eng, fn, reads=(), writes=()):
        self._waits(eng, self._deps(reads, writes))
        self.cnt[eng] += 1
        sem = self.sem[eng]
        tok = (sem, self.cnt[eng], eng)
        self.lists[eng].append(lambda e, fn=fn, sem=sem: fn(e).then_inc(sem, 1))
        self._update(tok, reads, writes)

    def dma(self, q, fn, reads=(), writes=()):
        deps = self._deps(reads, writes)
        i = self.dcnt[q]
        self.dcnt[q] += 1
        sem = self.dq[q][i % self.K]
        if i >= self.K:
            deps.append((sem, 16 * (i // self.K), "dma"))
        self._waits(q, deps)
        tok = (sem, 16 * (i // self.K + 1), "dma")
        self.lists[q].append(lambda e, fn=fn, sem=sem: fn(e).then_inc(sem, 16))
        self._update(tok, reads, writes)

    def transfer(self, olds, news):
        toks = {}
        for o in olds:
            for t in ([o.w] if o.w is not None else []) + list(o.rs.values()):
                key = id(t[0])
                if key not in toks or toks[key][1] < t[1]:
                    toks[key] = t
        for n in news:
            n.w = None
            n.rs = dict(toks)

    def finish(self):
        for q in ("sp", "pool", "act"):
            n = self.dcnt[q]
            for k in range(self.K):
                uses = (n - k + self.K - 1) // self.K if n > k else 0
                if uses > 0:
                    sem = self.dq[q][k]
                    self.lists["sp"].append(lambda e, sem=sem, v=16 * uses: e.wait_ge(sem, v))


def build_nc():
    nc = bass.Bass("TRN2", target_bir_lowering=False)

    def din(name, shape):
        return nc.dram_tensor(name, list(shape), F32, kind="ExternalInput").ap()

    def dout(name, shape):
        return nc.dram_tensor(name, list(shape), F32, kind="ExternalOutput").ap()

    xp_d = din("x_prompt", (SEQ, D))
    xs_d = din("x_sample", (NS, D))
    sta_d = din("state_conv_a", (DEPTH, NS, 2, 256))
    stp_d = din("state_pool", (DEPTH, NS, 15, 256))
    stc_d = din("state_conv_c", (DEPTH, NS, 30, 256))
    stf_d = din("state_conv_ffn", (DEPTH, NS, 2, 2 * DFF))
    norm_mix_d = din("norm_mix", (DEPTH, D))
    w_in_d = din("w_in", (DEPTH, D, 2048))
    conv_a_w_d = din("conv_a_w", (DEPTH, 3, 256))
    pool_w_d = din("pool_w", (DEPTH, 4, 64, 64))
    pool_scale_d = din("pool_scale", (DEPTH, 256))
    conv_c_w_d = din("conv_c_w", (DEPTH, 31, 256))
    conv_c_b_d = din("conv_c_b", (DEPTH, 256))
    ln_c_g_d = din("ln_c_g", (DEPTH, 256))
    ln_c_b_d = din("ln_c_b", (DEPTH, 256))
    ln_d_g_d = din("ln_d_g", (DEPTH, 256))
    ln_d_b_d = din("ln_d_b", (DEPTH, 256))
    w_s_d = din("w_s", (DEPTH, 4, 128, 128))
    b_s_d = din("b_s", (DEPTH, 4, 128))
    w_out_d = din("w_out", (DEPTH, D, D))
    norm_ffn_d = din("norm_ffn", (DEPTH, D))
    w_up_d = din("w_up", (DEPTH, D, 2 * DFF))
    conv_f_w_d = din("conv_f_w", (DEPTH, 3, 2 * DFF))
    w_down_d = din("w_down", (DEPTH, DFF, D))
    norm_final_d = din("norm_final", (D,))

    yp_o = dout("y_prompt", (SEQ, D))
    ys_o = dout("y_sample", (NS, D))
    pa_o = dout("new_conv_a_prompt", (DEPTH, 2, 256))
    pp_o = dout("new_pool_prompt", (DEPTH, 15, 256))
    pc_o = dout("new_conv_c_prompt", (DEPTH, 30, 256))
    pf_o = dout("new_conv_ffn_prompt", (DEPTH, 2, 2 * DFF))
    sa_o = dout("new_conv_a_sample", (DEPTH, NS, 2, 256))
    sp_o = dout("new_pool_sample", (DEPTH, NS, 15, 256))
    sc_o = dout("new_conv_c_sample", (DEPTH, NS, 30, 256))
    sf_o = dout("new_conv_ffn_sample", (DEPTH, NS, 2, 2 * DFF))
    sv_o = dout("new_chunk_v_sample", (DEPTH, NS, 256))

    stack = ExitStack()
    with stack:
        def sb(name, shape, dt=F32):
            return stack.enter_context(nc.sbuf_tensor(name, list(shape), dt))

        def ps(name, shape, dt=F32):
            return stack.enter_context(nc.psum_tensor(name, list(shape), dt))

        S = Sched(nc, stack)

        X = sb("X", (128, 8, TT))
        H = sb("H", (128, 8, 1040), BF16)
        ARENA = sb("ARENA", (128, 22 * 1040), BF16)
        RING = [sb("RING%d" % i, (128, 4096), BF16) for i in range(3)]
        NSCR = 5
        SCRW = 1072
        SCR = [sb("SCR%d" % i, (128, SCRW)) for i in range(NSCR)]
        PARAM = sb("PARAM", (128, DEPTH, NPAR))
        STG = sb("STG", (128, 1024))
        PSTG = STG[:, 0:256].rearrange("p (a b) -> p a b", a=2)
        PSTG2 = STG[:, 256:384]
        DIAG = sb("DIAG", (128, 31, 128), BF16)
        WG = sb("WG", (128, 2, 128), BF16)
        WST = sb("WST", (128, 4, 128), BF16)
        BSB = sb("BSB", (128, 2, 128))
        WS00 = sb("WS00", (128, 2))
        STA = sb("STA", (128, 2, NS, 2))
        STP = sb("STP", (128, 2, NS, 15))
        STC = sb("STC", (128, 2, NS, 30))
        STF = sb("STF", (128, NFC, NS, 2))
        CA_T = sb("CA_T", (128, 2, 2))
        P_T = sb("P_T", (128, 2, 15))
        GLU_T = sb("GLU_T", (128, 2, 30))
        UP_T = sb("UP_T", (128, 2, NFC))
        UPS = sb("UPS", (128, NFC, NS))
        NEWS = sb("NEWS", (128, 4, 2, NS))
        IDF = sb("IDF", (128, 128))
        IDB = sb("IDB", (128, 128), BF16)
        ONESB = sb("ONESB", (128, 128), BF16)
        MASKT = sb("MASKT", (128, 128))
        IOTA_I = sb("IOTA_I", (128, 128))
        PIDX = sb("PIDX", (128, 1))
        EPS_T = sb("EPS_T", (128, 1))
        SQD = sb("SQD", (128, 1))
        CORR = sb("CORR", (128, 2, 16))
        NFIN = sb("NFIN", (128, 8))
        OSTG = [sb("OSTG%d" % i, (128, 128)) for i in range(2)]
        SMALL = sb("SMALL", (128, 8, NS))

        PA = ps("PA", (128, 1536))
        PB = ps("PB", (128, 1536))
        PC = ps("PC", (128, 512))
        PD = ps("PD", (128, 512))

        Y = ARENA[:, 0:8 * 1040].rearrange("p (c t) -> p c t", c=8)
        o = 8 * 1040
        GLUB = ARENA[:, o:o + 2 * 1072].rearrange("p (c t) -> p c t", c=2)
        o += 2 * 1072
        POOLB = ARENA[:, o:o + 2 * 1040].rearrange("p (c t) -> p c t", c=2)
        VB = POOLB
        o += 2 * 1040
        ASCR = []
        ascr_offs = []
        for i in range(4):
            ASCR.append(ARENA[:, o:o + 2 * SCRW].bitcast(F32))
            ascr_offs.append(o)
            o += 2 * SCRW
        assert o <= 22 * 1040, o
        ACTB = ARENA[:, 0:22 * 1040].rearrange("p (c t) -> p c t", c=22)
        SQ = ARENA[:, 0:8 * 1040].rearrange("p (c t) -> p c t", c=8)

        rXs = [[Res("X%d_%d" % (c, sg)) for c in range(8)] for sg in range(2)]
        rX = rXs[0] + rXs[1]
        rXhs = [[[Res("Xh%d_%d_%d" % (sg, c, h)) for h in range(2)] for c in range(8)] for sg in range(2)]
        rHch = [[Res("H%d_%d" % (c, h)) for h in range(2)] for c in range(8)]
        rHc = [r for pair in rHch for r in pair]
        rH0 = [rHch[c][0] for c in range(8)]
        rH1 = [rHch[c][1] for c in range(8)]
        rSQc = [Res("SQ%d" % c) for c in range(8)]
        rYc = [Res("Y%d" % c) for c in range(8)]
        rGLUBc = [Res("GLUB0"), Res("GLUB1")]
        rPOOLB = Res("POOLB")
        rVB = rPOOLB
        rASCR = [Res("ASCR%d" % i) for i in range(4)]
        rACTBc = [Res("ACTB%d" % j) for j in range(22)]
        rSQ = Res("SQ")
        rY45 = [[Res("Y%d_h%d" % (4 + c, h)) for h in range(2)] for c in range(2)]
        arena_mix = rYc + rGLUBc + [rPOOLB] + rASCR + rY45[0] + rY45[1]
        rRING = [Res("RING%d" % i) for i in range(3)]
        rSCR = [Res("SCR%d" % i) for i in range(NSCR)]
        rPA, rPB, rPC, rPD = Res("PA"), Res("PB"), Res("PC"), Res("PD")
        rPARAM, rPSTG, rDIAG, rWG, rWSF, rWST, rBSB, rWS00 = (Res(n) for n in ("PARAM", "PSTG", "DIAG", "WG", "WSF", "WST", "BSB", "WS00"))
        rSTA, rSTP, rSTC, rSTF = Res("STA"), Res("STP"), Res("STC"), Res("STF")
        rUP_T, rNEWS = Res("UP_T"), Res("NEWS")
        rCA_Tc = [Res("CA_T0"), Res("CA_T1")]
        rP_Tc = [Res("P_T0"), Res("P_T1")]
        rGLU_Tc = [Res("GLU_T0"), Res("GLU_T1")]
        rUPSc = [Res("UPS%d" % i) for i in range(NFC)]
        rCONST = Res("CONST")
        rSQD = Res("SQD")
        rPARAMl = [Res("PARAM%d" % i) for i in range(DEPTH)]
        rR2 = Res("R2")
        rOSTG = [Res("OSTG0"), Res("OSTG1")]
        rSMALL = Res("SMALL")
        rNFIN = Res("NFIN")
        rDRAM = Res("DRAM")

        scr_all = [(SCR[i][:, :], rSCR[i]) for i in range(NSCR)]
        scr_mix = scr_all + [(ASCR[i], rASCR[i]) for i in range(4)]

        ostg_i = [0]

        S.op("pool", lambda e: e.iota(IOTA_I[:, :], [[1, 128]], base=0, channel_multiplier=0, allow_small_or_imprecise_dtypes=True), writes=[rCONST])
        S.op("pool", lambda e: e.iota(PIDX[:, :], [[0, 1]], base=0, channel_multiplier=1, allow_small_or_imprecise_dtypes=True), writes=[rCONST])
        S.op("pool", lambda e: e.memset(ONESB[:, :], 1.0), writes=[rCONST])
        S.op("pool", lambda e: e.memset(EPS_T[:, :], EPS), writes=[rCONST])
        S.op("pool", lambda e: e.memset(WG[:, :, :], 0.0), writes=[rWG])
        S.op("dve", lambda e: e.tensor_scalar(out=IDF[:, :], in0=IOTA_I[:, :], scalar1=PIDX[:, 0:1], scalar2=None, op0=ALU.is_equal), reads=[rCONST], writes=[rCONST])
        S.op("dve", lambda e: e.tensor_copy(out=IDB[:, :], in_=IDF[:, :]), reads=[rCONST], writes=[rCONST])
        S.op("dve", lambda e: e.tensor_scalar(out=MASKT[:, :], in0=IOTA_I[:, :], scalar1=PIDX[:, 0:1], scalar2=None, op0=ALU.is_ge), reads=[rCONST], writes=[rCONST])
        for c in range(2):
            for hf in range(2):
                win = float(WINS[2 * c + hf])
                S.op("dve", lambda e, c=c, hf=hf, win=win: e.tensor_scalar(out=CORR[hf * 64:(hf + 1) * 64, c, :], in0=IOTA_I[hf * 64:(hf + 1) * 64, 0:16],
                                                                        scalar1=1.0, scalar2=win, op0=ALU.add, op1=ALU.min), reads=[rCONST], writes=[rCONST])
        S.op("dve", lambda e: e.reciprocal(out=CORR[:, :, :], in_=CORR[:, :, :]), reads=[rCONST], writes=[rCONST])

        def load_params_dma(l, q="sp"):
            rows = [
                (norm_mix_d[l].rearrange("(c p) -> c p", p=128), 8),
                (norm_ffn_d[l].rearrange("(c p) -> c p", p=128), 8),
                (conv_a_w_d[l].rearrange("k (c p) -> (k c) p", p=128), 6),
                (pool_scale_d[l].rearrange("(c p) -> c p", p=128), 2),
                (conv_c_w_d[l].rearrange("k (c p) -> (k c) p", p=128), 62),
                (conv_c_b_d[l].rearrange("(c p) -> c p", p=128), 2),
                (ln_c_g_d[l].rearrange("(c p) -> c p", p=128), 2),
                (ln_c_b_d[l].rearrange("(c p) -> c p", p=128), 2),
                (ln_d_g_d[l].rearrange("(c p) -> c p", p=128), 2),
                (ln_d_b_d[l].rearrange("(c p) -> c p", p=128), 2),
            ]
            r0 = 0
            for (ap, n) in rows:
                S.dma(q, lambda e, ap=ap, r0=r0, n=n: e.dma_start(out=PSTG[r0:r0 + n, 0, :], in_=ap), writes=[rPSTG])
                r0 += n
            cf = conv_f_w_d[l].rearrange("k (c p) -> (k c) p", p=128)
            S.dma(q, lambda e: e.dma_start(out=PSTG[0:128, 1, :], in_=cf[0:128, :]), writes=[rPSTG])
            S.dma(q, lambda e: e.dma_start(out=PSTG2[0:4, :], in_=cf[128:132, :]), writes=[rPSTG])

        def load_params_compute(l, ps=None):
            PX, rPX = ps if ps is not None else (PD, rPD)
            S.op("pe", lambda e: e.transpose(PX[:, 0:96], PSTG[0:96, 0, :], IDF[0:96, 0:96]), reads=[rPSTG, rCONST], writes=[rPX])
            S.op("pe", lambda e: e.transpose(PX[:, 96:224], PSTG[0:128, 1, :], IDF[:, :]), reads=[rPSTG, rCONST], writes=[rPX])
            S.op("pe", lambda e: e.transpose(PX[:, 224:228], PSTG2[0:4, :], IDF[0:4, 0:4]), reads=[rPSTG, rCONST], writes=[rPX])
            S.op("act", lambda e: e.activation(out=PARAM[:, l, :], in_=PX[:, 0:NPAR], func=AF.Copy), reads=[rPX], writes=[rPARAMl[l]])

        def load_params(l):
            load_params_dma(l)
            load_params_compute(l)

        def par(l, col):
            return PARAM[:, l, col:col + 1]


        for tb in range(17):
            ntok = 128 if tb < 16 else NS
            src = xp_d[tb * 128:(tb + 1) * 128, :] if tb < 16 else xs_d[:, :]
            sap, sres = scr_all[tb % NSCR]
            if tb == NSCR:
                load_params_dma(0, q="sp")
            S.dma("sp", lambda e, sap=sap, src=src, ntok=ntok: e.dma_start(out=sap[0:ntok, 0:1024], in_=src), writes=[sres])
            pt, rpt = (PA, rPA) if tb % 2 == 0 else (PB, rPB)

            def tr(e, sap=sap, pt=pt, ntok=ntok):
                for c in range(8):
                    ins = e.transpose(pt[:, c * 128:c * 128 + ntok], sap[0:ntok, c * 128:(c + 1) * 128], IDF[0:ntok, 0:ntok])
                return ins
            S.op("pe", tr, reads=[sres, rCONST], writes=[rpt])
            eng = "act" if tb % 2 == 0 else "dve"
            pv = pt[:, 0:1024].rearrange("p (c t) -> p c t", c=8)[:, :, 0:ntok]
            if eng == "act":
                S.op("act", lambda e, pv=pv, tb=tb, ntok=ntok: e.activation(out=X[:, :, tb * 128:tb * 128 + ntok], in_=pv, func=AF.Copy), reads=[rpt], writes=rX)
            else:
                S.op("dve", lambda e, pv=pv, tb=tb, ntok=ntok: e.tensor_copy(out=X[:, :, tb * 128:tb * 128 + ntok], in_=pv), reads=[rpt], writes=rX)

        load_params_compute(0)
        S.dma("sp", lambda e: e.dma_start(out=PSTG[0:8, 0, :], in_=norm_final_d.rearrange("(c p) -> c p", p=128)), writes=[rPSTG])
        S.op("pe", lambda e: e.transpose(PD[:, 0:8], PSTG[0:8, 0, :], IDF[0:8, 0:8]), reads=[rPSTG, rCONST], writes=[rPD])
        S.op("act", lambda e: e.activation(out=NFIN[:, :], in_=PD[:, 0:8], func=AF.Copy), reads=[rPD], writes=[rNFIN])

        ring_i = [0]

        def ring_load(parts):
            i = ring_i[0] % 3
            ring_i[0] += 1
            slot, res = RING[i], rRING[i]
            for (dst_fn, src) in parts:
                S.dma("pool", lambda e, dst=dst_fn(slot), src=src: e.dma_start(out=dst, in_=src), writes=[res])
            return slot, res

        psl_i = [0]

        def next_pslot():
            i = psl_i[0] % 2
            psl_i[0] += 1
            return (PA, rPA) if i == 0 else (PB, rPB)

        def seg_tiles(seg):
            t = [(0, 512), (512, 512)]
            if seg == 1:
                t.append((1024, NS))
            return t

        def mm_chunk(pt, rpt, lhs_fn, nk, rhs_fn, tiles, reads, tile_reads=None):
            def mk(tl):
                def f(e):
                    ins = None
                    for (c0, n) in tl:
                        for k in range(nk):
                            ins = e.matmul(pt[:, c0:c0 + n], lhsT=lhs_fn(k), rhs=rhs_fn(k, c0, n), start=(k == 0), stop=(k == nk - 1))
                    return ins
                return f
            if tile_reads is None:
                S.op("pe", mk(tiles), reads=reads, writes=[rpt])
            else:
                for ti, tl in enumerate(tiles):
                    S.op("pe", mk([tl]), reads=reads + tile_reads[ti], writes=[rpt])

        def tr_out(src_ap, n, dst_ap, reads, eng="act"):
            i = ostg_i[0] % 2
            ostg_i[0] += 1
            st, rst = OSTG[i], rOSTG[i]
            PX, rPX = PC, rPC
            S.op("pe", lambda e: e.transpose(PX[0:n, 0:128], src_ap, IDF[:, :]), reads=reads + [rCONST], writes=[rPX])
            if eng == "act":
                S.op("act", lambda e: e.activation(out=st[0:n, :], in_=PX[0:n, 0:128], func=AF.Copy), reads=[rPX], writes=[rst])
            else:
                S.op("dve", lambda e: e.tensor_copy(out=st[0:n, :], in_=PX[0:n, 0:128]), reads=[rPX], writes=[rst])
            S.dma("sp", lambda e: e.dma_start(out=dst_ap, in_=st[0:n, :]), reads=[rst])

        rSTGO = [Res("STGO0"), Res("STGO1")]
        stgo_i = [0]

        def tr_out_multi(src_aps, n, dsts, reads):
            k = len(src_aps)
            st = STG[0:n, 512:512 + k * 128]
            rst = rSTGO[0]
            PX, rPX = PD, rPD

            def trm(e):
                ins = None
                for j, sap in enumerate(src_aps):
                    ins = e.transpose(PX[0:n, j * 128:(j + 1) * 128], sap, IDF[:, :])
                return ins
            S.op("pe", trm, reads=reads + [rCONST], writes=[rPX])
            S.op("act", lambda e: e.activation(out=st, in_=PX[0:n, 0:k * 128], func=AF.Copy), reads=[rPX], writes=[rst])
            for (dap, c0, ncol) in dsts:
                S.dma("sp", lambda e, dap=dap, c0=c0, ncol=ncol: e.dma_start(out=dap, in_=STG[0:n, 512 + c0:512 + c0 + ncol]), reads=[rst])

        rms_state = {}

        def rms_stats(seg, W, xoff, sql, Rap, rRh, phase="all"):
            if phase in ("sq", "all"):
                S.op("act", lambda e: e.activation(out=SQD[:, 0:1], in_=EPS_T[:, 0:1], func=AF.Sqrt), reads=[rCONST], writes=[rSQD])
                for c in range(8):
                    S.op("act", lambda e, c=c: e.activation(out=sql[c][0][:, 0:W], in_=X[:, c, xoff:xoff + W], func=AF.Square), reads=[rXs[seg][c]], writes=[sql[c][1]])
            if phase == "sq":
                return
            pt, rpt = next_pslot()
            tl = seg_tiles(seg)

            def f1(e):
                ins = None
                for (c0, n) in tl:
                    for k in range(7):
                        ins = e.matmul(pt[:, c0:c0 + n], lhsT=ONESB[:, :], rhs=sql[k][0][:, c0:c0 + n], start=(k == 0), stop=False)
                return ins

            def f2(e):
                ins = None
                for (c0, n) in tl:
                    ins = e.matmul(pt[:, c0:c0 + n], lhsT=ONESB[:, :], rhs=sql[7][0][:, c0:c0 + n], start=False, stop=True)
                return ins
            S.op("pe", f1, reads=list(dict.fromkeys([sql[k][1] for k in range(7)])) + [rCONST], writes=[rpt])
            S.op("pe", f2, reads=[sql[7][1], rCONST], writes=[rpt])
            S.op("act", lambda e: e.activation(out=Rap[:, 0:W], in_=pt[:, 0:W], func=AF.Sqrt, bias=EPS_T[:, 0:1], scale=1.0 / D), reads=[rpt, rCONST], writes=rRh)

        def sq_bufs(bufs):
            out = []
            for (ap, r) in bufs:
                b = ap.bitcast(BF16)
                out.append((b[:, 0:SCRW], r))
                out.append((b[:, SCRW:2 * SCRW], r))
            return out

        def norm_apply(W, xoff, Rap, rRh, gcol, l, defer=False):
            seg = xoff // SEGW
            dq = []
            for h, (a, b) in enumerate(((0, 512), (512, W))):
                dq.append(lambda h=h, a=a, b=b: S.op("dve", lambda e: e.reciprocal(out=Rap[:, a:b], in_=Rap[:, a:b]), reads=[rRh[h]], writes=[rRh[h]]))
                for c in range(8):
                    dq.append(lambda h=h, a=a, b=b, c=c: S.op("dve", lambda e: e.scalar_tensor_tensor(out=H[:, c, a:b], in0=X[:, c, xoff + a:xoff + b], scalar=par(l, gcol + c), in1=Rap[:, a:b],
                                                                                                     op0=ALU.mult, op1=ALU.mult), reads=[rXs[seg][c], rRh[h], rPARAMl[l]], writes=[rHch[c][h]]))
            if defer:
                return dq
            for f_ in dq:
                f_()
            return []

        def ln_prep(src, c, W, tmpb, rtl, rcp=None):
            S.op("act", lambda e: e.activation(out=tmpb[:, c, 0:W], in_=src[0][:, 0:W], func=AF.Copy), reads=[src[1]], writes=rtl + ([rcp[c]] if rcp else []))
            S.op("act", lambda e: e.activation(out=tmpb[:, 2 + c, 0:W], in_=src[0][:, 0:W], func=AF.Square), reads=[src[1]], writes=rtl)

        def sqrt_preload():
            S.op("act", lambda e: e.activation(out=SQD[:, 0:1], in_=EPS_T[:, 0:1], func=AF.Sqrt), reads=[rCONST], writes=[rSQD])

        def ln_stats(srcs, W, seg, tmpb, rtmpb, mean_ap, rmean, rstd_ap, rrstd, prep_done=False, do_recip=True, rcp=None):
            rtl = rtmpb if isinstance(rtmpb, list) else [rtmpb]
            if not prep_done:
                sqrt_preload()
                for c in range(2):
                    ln_prep(srcs[c], c, W, tmpb, rtl)
            pt, rpt = next_pslot()
            mm_chunk(pt, rpt, lambda k: ONESB[:, :], 2, lambda k, c0, n: tmpb[:, k, c0:c0 + n], seg_tiles(seg), (list(rcp) if rcp else rtl) + [rCONST])
            S.op("act", lambda e: e.activation(out=mean_ap[:, 0:W], in_=pt[:, 0:W], func=AF.Copy, scale=1.0 / 256), reads=[rpt], writes=[rmean])
            pt2, rpt2 = next_pslot()
            mm_chunk(pt2, rpt2, lambda k: ONESB[:, :], 2, lambda k, c0, n: tmpb[:, 2 + k, c0:c0 + n], seg_tiles(seg), rtl + [rCONST])
            S.op("dve", lambda e: e.tensor_tensor(out=rstd_ap[:, 0:W], in0=mean_ap[:, 0:W], in1=mean_ap[:, 0:W], op=ALU.mult), reads=[rmean], writes=[rrstd])
            S.op("dve", lambda e: e.scalar_tensor_tensor(out=rstd_ap[:, 0:W], in0=pt2[:, 0:W], scalar=1.0 / 256, in1=rstd_ap[:, 0:W], op0=ALU.mult, op1=ALU.subtract),
                 reads=[rpt2, rrstd], writes=[rrstd])
            S.op("act", lambda e: e.activation(out=rstd_ap[:, 0:W], in_=rstd_ap[:, 0:W], func=AF.Sqrt, bias=EPS_T[:, 0:1], scale=1.0), reads=[rrstd, rCONST], writes=[rrstd])
            if do_recip:
                S.op("dve", lambda e: e.reciprocal(out=rstd_ap[:, 0:W], in_=rstd_ap[:, 0:W]), reads=[rrstd], writes=[rrstd])

        rSTG = rPSTG

        def prep_layer_dma(l, with_wsf=True):
            for c in range(2):
                for hf in range(2):
                    S.dma("pool", lambda e, c=c, hf=hf: e.dma_start(out=WG[hf * 64:(hf + 1) * 64, c, hf * 64:(hf + 1) * 64], in_=pool_w_d[l, 2 * c + hf, :, :]), writes=[rWG])
            if with_wsf:
                WSF = STG[:, 0:512].rearrange("p (h j) -> p h j", h=4)
                S.dma("sp", lambda e: e.dma_start(out=WSF, in_=w_s_d[l].rearrange("h i j -> i h j")), writes=[rSTG])
            for c in range(2):
                for hf in range(2):
                    hd = 2 * c + hf
                    S.dma("sp", lambda e, c=c, hf=hf, hd=hd: e.dma_start(out=BSB[hf * 64:(hf + 1) * 64, c, :], in_=b_s_d[l, hd, :].partition_broadcast(64)), writes=[rBSB])
                    S.dma("sp", lambda e, c=c, hf=hf, hd=hd: e.dma_start(out=WS00[hf * 64:(hf + 1) * 64, c:c + 1], in_=w_s_d[l, hd, 0, 0:1].partition_broadcast(64)), writes=[rWS00])

        def wsf_dma_hi(l):
            WSF = STG[:, 512:1024].rearrange("p (h j) -> p h j", h=4)
            S.dma("sp", lambda e: e.dma_start(out=WSF, in_=w_s_d[l].rearrange("h i j -> i h j")), writes=[rSTGO[0], rSTGO[1]])

        def prep_layer_compute(l, hi=False):
            WSF = (STG[:, 512:1024] if hi else STG[:, 0:512]).rearrange("p (h j) -> p h j", h=4)
            rsrc = [rSTGO[0], rSTGO[1]] if hi else [rSTG]
            for hd in range(4):
                S.op("pe", lambda e, hd=hd: e.transpose(PD[:, hd * 128:(hd + 1) * 128], WSF[:, hd, :], IDF[:, :]), reads=rsrc + [rCONST], writes=[rPD])
            S.op("dve", lambda e: e.tensor_tensor(out=WST[:, :, :], in0=PD[:, 0:512].rearrange("p (h i) -> p h i", h=4),
                                                  in1=MASKT[:, :].unsqueeze(1).to_broadcast([128, 4, 128]), op=ALU.mult), reads=[rPD, rCONST], writes=[rWST])

        def prep_layer(l):
            prep_layer_dma(l)
            prep_layer_compute(l)

        def state_batches(l):
            batches = []

            def b_sta():
                S.dma("sp", lambda e: e.dma_start(out=STG[0:32, 0:256], in_=sta_d[l].rearrange("s r c -> (s r) c")), writes=[rSTG])
                for c in range(2):
                    S.op("pe", lambda e, c=c: e.transpose(PC[:, c * 32:(c + 1) * 32], STG[0:32, c * 128:(c + 1) * 128], IDF[0:32, 0:32]), reads=[rSTG, rCONST], writes=[rPC])
                S.op("act", lambda e: e.activation(out=STA[:, :, :, :].rearrange("p c s r -> p (c s r)"), in_=PC[:, 0:64], func=AF.Copy), reads=[rPC], writes=[rSTA])
            batches.append(b_sta)

            def b_stp():
                stp_rows = stp_d[l].rearrange("s r c -> (s r) c")
                for t in range(2):
                    S.dma("sp", lambda e, t=t: e.dma_start(out=STG[0:120, t * 256:(t + 1) * 256], in_=stp_rows[t * 120:(t + 1) * 120, :]), writes=[rSTG])
                for t in range(2):
                    for c in range(2):
                        S.op("pe", lambda e, t=t, c=c: e.transpose(PC[:, c * 240 + t * 120:c * 240 + (t + 1) * 120], STG[0:120, t * 256 + c * 128:t * 256 + (c + 1) * 128], IDF[0:120, 0:120]),
                             reads=[rSTG, rCONST], writes=[rPC])
                S.op("act", lambda e: e.activation(out=STP[:, :, :, :].rearrange("p c s r -> p (c s r)"), in_=PC[:, 0:480], func=AF.Copy), reads=[rPC], writes=[rSTP])
            batches.append(b_stp)

            def mk_stc(c):
                def b_stc():
                    stc_rows = stc_d[l].rearrange("s r c -> (s r) c")
                    for t in range(4):
                        S.dma("sp", lambda e, t=t: e.dma_start(out=STG[0:120, t * 128:(t + 1) * 128], in_=stc_rows[t * 120:(t + 1) * 120, c * 128:(c + 1) * 128]), writes=[rSTG])
                    for t in range(4):
                        S.op("pe", lambda e, t=t: e.transpose(PC[:, t * 120:(t + 1) * 120], STG[0:120, t * 128:(t + 1) * 128], IDF[0:120, 0:120]), reads=[rSTG, rCONST], writes=[rPC])
                    S.op("act", lambda e: e.activation(out=STC[:, c, :, :].rearrange("p s r -> p (s r)"), in_=PC[:, 0:480], func=AF.Copy), reads=[rPC], writes=[rSTC])
                return b_stc
            batches.append(mk_stc(0))
            batches.append(mk_stc(1))

            def mk_stf(b):
                def b_stf():
                    stf_rows = stf_d[l].rearrange("s r c -> (s r) c")
                    c0 = b * 8
                    nch = min(8, NFC - c0)
                    S.dma("sp", lambda e: e.dma_start(out=STG[0:32, 0:nch * 128], in_=stf_rows[:, c0 * 128:(c0 + nch) * 128]), writes=[rSTG, rSTGO[0], rSTGO[1]])

                    def trf(e):
                        for j in range(nch):
                            ins = e.transpose(PC[:, j * 32:(j + 1) * 32], STG[0:32, j * 128:(j + 1) * 128], IDF[0:32, 0:32])
                        return ins
                    S.op("pe", trf, reads=[rSTG, rCONST], writes=[rPC])
                    S.op("act", lambda e: e.activation(out=STF[:, c0:c0 + nch, :, :].rearrange("p c s r -> p (c s r)"), in_=PC[:, 0:nch * 32], func=AF.Copy),
                         reads=[rPC], writes=[rSTF])
                return b_stf
            for b in range(6):
                batches.append(mk_stf(b))
            return batches

        prep_layer(0)

        def build_diag(l, c):
            S.op("dve", lambda e: e.tensor_tensor(out=DIAG[:, :, :], in0=IDB[:, :].unsqueeze(1).to_broadcast([128, 31, 128]),
                                                  in1=PARAM[:, l, C_CCW + c:C_CCW + c + 62:2].unsqueeze(2).to_broadcast([128, 31, 128]), op=ALU.mult),
                 reads=[rCONST, rPARAMl[l]], writes=[rDIAG])

        hoisted = set()

        def mixer_norm(l, seg, phase="all"):
            W = SEGW + (NS if seg == 1 else 0)
            xoff = seg * SEGW
            Rap, rR = scr_all[0]
            rRh = [rR, rR2]
            sql = sq_bufs([scr_all[1], scr_all[2], scr_all[3], scr_all[4]])
            if phase in ("sq", "all"):
                rms_stats(seg, W, xoff, sql, Rap, rRh, phase="sq")
            if phase in ("rest", "all"):
                rms_stats(seg, W, xoff, sql, Rap, rRh, phase="rest")
                dq = norm_apply(W, xoff, Rap, rRh, C_NM, l, defer=(phase == "rest"))
                dq.append(lambda: S.transfer([rR, rR2], [rR]))
                dq.append(lambda: build_diag(l, 0))
                if phase == "rest":
                    return dq
                for f_ in dq:
                    f_()
            return []

        def do_layer(l):
            rPARAM = rPARAMl[l]
            sbatches = state_batches(l)
            S.dma("sp", lambda e: e.dma_start(out=sa_o[l, :, 0:1, :], in_=sta_d[l, :, 1:2, :]))
            S.dma("sp", lambda e: e.dma_start(out=sp_o[l, :, 0:14, :], in_=stp_d[l, :, 1:15, :]))
            S.dma("sp", lambda e: e.dma_start(out=sc_o[l, :, 0:29, :], in_=stc_d[l, :, 1:30, :]))
            S.dma("sp", lambda e: e.dma_start(out=sf_o[l, :, 0:1, :], in_=stf_d[l, :, 1:2, :]))
            S.op("dve", lambda e: e.memset(CA_T[:, :, :], 0.0), writes=rCA_Tc)
            S.op("dve", lambda e: e.memset(P_T[:, :, :], 0.0), writes=rP_Tc)
            S.op("dve", lambda e: e.memset(GLU_T[:, :, :], 0.0), writes=rGLU_Tc)
            S.op("dve", lambda e: e.memset(UP_T[:, :, :], 0.0), writes=[rUP_T])

            def do_seg(seg):
                rX = rXs[seg]
                W = SEGW + (NS if seg == 1 else 0)
                xoff = seg * SEGW
                tiles = seg_tiles(seg)
                smp = (seg == 1)
                S.transfer(rACTBc, arena_mix)
                B_ = scr_mix
                Rap, rR = B_[0]
                if (l, seg) not in hoisted:
                    mixer_norm(l, seg)
                first_chunk = [True]
                wv = w_in_d[l].rearrange("(k p) n -> p k n", p=128)

                def load_in(q):
                    return ring_load([(lambda sl: sl[:, :].rearrange("p (k n) -> p k n", k=8), wv[:, :, q * 512:(q + 1) * 512])])

                def zchunk(slot, rslot, j):
                    sv = slot[:, :].rearrange("p (k n) -> p k n", k=8)
                    pt, rpt = next_pslot()
                    if first_chunk[0]:
                        first_chunk[0] = False
                        mm_chunk(pt, rpt, lambda k: sv[:, k, j * 128:(j + 1) * 128], 8, lambda k, c0, n: H[:, k, c0:c0 + n], tiles, [rslot],
                                 tile_reads=[rH0] + [rH1] * (len(tiles) - 1))
                    else:
                        mm_chunk(pt, rpt, lambda k: sv[:, k, j * 128:(j + 1) * 128], 8, lambda k, c0, n: H[:, k, c0:c0 + n], tiles, [rslot] + rHc)
                    return pt, rpt

                sl0, rsl0 = load_in(0)
                sl1, rsl1 = load_in(1)
                AB = [B_[1], B_[2]]
                AC = [B_[3], B_[4]]
                CA = [B_[5], B_[6]]
                for c in range(2):
                    pt, rpt = zchunk(sl0, rsl0, c)
                    S.op("act", lambda e, pt=pt, c=c: e.activation(out=AB[c][0][:, 0:W], in_=pt[:, 0:W], func=AF.Copy), reads=[rpt], writes=[AB[c][1]])
                for c in range(2):
                    pt, rpt = zchunk(sl0, rsl0, 2 + c)
                    S.op("act", lambda e, pt=pt, c=c: e.activation(out=AC[c][0][:, 0:W], in_=pt[:, 0:W], func=AF.Copy), reads=[rpt], writes=[AC[c][1]])
                for c in range(2):
                    pt, rpt = zchunk(sl1, rsl1, c)
                    ca, rca = CA[c]
                    acc, racc = AC[c]
                    S.op("dve", lambda e, c=c, ca=ca: e.tensor_copy(out=ca[:, 0:2], in_=CA_T[:, c, :]), reads=[rCA_Tc[c]], writes=[rca])
                    S.op("dve", lambda e, pt=pt, ca=ca, acc=acc: e.tensor_tensor(out=ca[:, 2:2 + W], in0=pt[:, 0:W], in1=acc[:, 0:W], op=ALU.mult), reads=[rpt, racc], writes=[rca])
                    S.op("dve", lambda e, ca=ca, acc=acc, c=c: e.tensor_scalar(out=acc[:, 0:SEGW], in0=ca[:, 2:2 + SEGW], scalar1=par(l, C_CAW + 4 + c), scalar2=None, op0=ALU.mult),
                         reads=[rca, rPARAM], writes=[racc])
                    S.op("dve", lambda e, ca=ca, acc=acc, c=c: e.scalar_tensor_tensor(out=acc[:, 0:SEGW], in0=ca[:, 1:1 + SEGW], scalar=par(l, C_CAW + 2 + c), in1=acc[:, 0:SEGW],
                                                                                      op0=ALU.mult, op1=ALU.add), reads=[rca, rPARAM, racc], writes=[racc])
                    S.op("dve", lambda e, ca=ca, acc=acc, c=c: e.scalar_tensor_tensor(out=acc[:, 0:SEGW], in0=ca[:, 0:SEGW], scalar=par(l, C_CAW + 0 + c), in1=acc[:, 0:SEGW],
                                                                                      op0=ALU.mult, op1=ALU.add), reads=[rca, rPARAM, racc], writes=[racc])
                    if smp:
                        cs = ca[:, 2 + SEGW:2 + W]
                        S.op("dve", lambda e, acc=acc, cs=cs, c=c: e.tensor_scalar(out=acc[:, SEGW:W], in0=cs, scalar1=par(l, C_CAW + 4 + c), scalar2=None, op0=ALU.mult),
                             reads=[rca, rPARAM], writes=[racc])
                        S.op("dve", lambda e, acc=acc, c=c: e.scalar_tensor_tensor(out=acc[:, SEGW:W], in0=STA[:, c, :, 1], scalar=par(l, C_CAW + 2 + c), in1=acc[:, SEGW:W],
                                                                                  op0=ALU.mult, op1=ALU.add), reads=[rSTA, rPARAM, racc], writes=[racc])
                        S.op("dve", lambda e, acc=acc, c=c: e.scalar_tensor_tensor(out=acc[:, SEGW:W], in0=STA[:, c, :, 0], scalar=par(l, C_CAW + 0 + c), in1=acc[:, SEGW:W],
                                                                                  op0=ALU.mult, op1=ALU.add), reads=[rSTA, rPARAM, racc], writes=[racc])
                        S.op("dve", lambda e, cs=cs, c=c: e.tensor_copy(out=NEWS[:, 0, c, :], in_=cs), reads=[rca], writes=[rNEWS])
                    S.op("dve", lambda e, acc=acc, c=c: e.tensor_tensor(out=Y[:, c, 0:W], in0=acc[:, 0:W], in1=AB[c][0][:, 0:W], op=ALU.mult), reads=[racc, AB[c][1]], writes=[rYc[c]])
                    S.op("dve", lambda e, ca=ca, c=c: e.tensor_copy(out=CA_T[:, c, :], in_=ca[:, SEGW:SEGW + 2]), reads=[rca], writes=[rCA_Tc[c]])

                PBUF = [B_[7], B_[8]]
                TMP = [B_[3], B_[4]]
                for c in range(2):
                    pt, rpt = zchunk(sl1, rsl1, 2 + c)
                    pb, rpb = PBUF[c]
                    S.op("act", lambda e, c=c, pb=pb: e.activation(out=pb[:, 0:15], in_=P_T[:, c, :], func=AF.Copy), reads=[rP_Tc[c]], writes=[rpb])
                    S.op("act", lambda e, pt=pt, pb=pb: e.activation(out=pb[:, 15:15 + W], in_=pt[:, 0:W], func=AF.Copy), reads=[rpt], writes=[rpb])
                    t1, rt1 = TMP[0]
                    t2, rt2 = TMP[1]
                    L = 15 + SEGW
                    S.op("dve", lambda e, pb=pb, t1=t1: e.tensor_tensor(out=t1[:, 1:L], in0=pb[:, 1:L], in1=pb[:, 0:L - 1], op=ALU.add), reads=[rpb], writes=[rt1])
                    S.op("dve", lambda e, t1=t1, t2=t2: e.tensor_tensor(out=t2[:, 3:L], in0=t1[:, 3:L], in1=t1[:, 1:L - 2], op=ALU.add), reads=[rt1], writes=[rt2])
                    if c == 1:
                        S.op("dve", lambda e, t1=t1, t2=t2: e.tensor_tensor(out=t1[:, 7:L], in0=t2[:, 7:L], in1=t2[:, 3:L - 4], op=ALU.add), reads=[rt2, rt1], writes=[rt1])
                        S.op("dve", lambda e, t1=t1, t2=t2: e.tensor_tensor(out=t2[:, 15:L], in0=t1[:, 15:L], in1=t1[:, 7:L - 8], op=ALU.add), reads=[rt1, rt2], writes=[rt2])
                    for hf, (sa, rsa) in enumerate(((t1, rt1), (t2, rt2))):
                        win = WINS[2 * c + hf]
                        pr = slice(hf * 64, (hf + 1) * 64)
                        S.op("dve", lambda e, sa=sa, pb=pb, pr=pr, win=win, c=c: e.scalar_tensor_tensor(out=POOLB[pr, c, 0:SEGW], in0=sa[pr, 15:15 + SEGW], scalar=1.0 / win,
                                                                                                    in1=pb[pr, 15:15 + SEGW], op0=ALU.mult, op1=ALU.subtract),
                             reads=[rsa, rpb], writes=[rPOOLB])
                        if seg == 0:
                            sm = SMALL[pr, 0, 0:15]
                            S.op("dve", lambda e, sa=sa, pr=pr, c=c, sm=sm: e.tensor_tensor(out=sm, in0=sa[pr, 15:30], in1=CORR[pr, c, 0:15], op=ALU.mult), reads=[rsa, rCONST], writes=[rSMALL])
                            S.op("dve", lambda e, pb=pb, pr=pr, c=c, sm=sm: e.tensor_tensor(out=POOLB[pr, c, 0:15], in0=sm, in1=pb[pr, 15:30], op=ALU.subtract), reads=[rSMALL, rpb], writes=[rPOOLB])
                        if smp:
                            sm = SMALL[pr, 1, :]
                            ps_ = pb[pr, 15 + SEGW:15 + W]
                            S.op("dve", lambda e, pr=pr, c=c, win=win, sm=sm: e.tensor_reduce(out=sm, in_=STP[pr, c, :, 16 - win:15], axis=AX.X, op=ALU.add), reads=[rSTP], writes=[rSMALL])
                            S.op("dve", lambda e, sm=sm, ps_=ps_: e.tensor_tensor(out=sm, in0=sm, in1=ps_, op=ALU.add), reads=[rSMALL, rpb], writes=[rSMALL])
                            S.op("dve", lambda e, sm=sm, ps_=ps_, pr=pr, c=c, win=win: e.scalar_tensor_tensor(out=POOLB[pr, c, SEGW:W], in0=sm, scalar=1.0 / win, in1=ps_,
                                                                                                        op0=ALU.mult, op1=ALU.subtract), reads=[rSMALL, rpb], writes=[rPOOLB])
                    if smp:
                        S.op("dve", lambda e, pb=pb, c=c: e.tensor_copy(out=NEWS[:, 1, c, :], in_=pb[:, 15 + SEGW:15 + W]), reads=[rpb], writes=[rNEWS])
                    S.op("dve", lambda e, pb=pb, c=c: e.tensor_copy(out=P_T[:, c, :], in_=pb[:, SEGW:SEGW + 15]), reads=[rpb], writes=[rP_Tc[c]])

                sl2, rsl2 = load_in(2)
                SG = [B_[0], B_[5]]
                t3, rt3 = B_[6]
                for c in range(2):
                    pt, rpt = zchunk(sl2, rsl2, 2 + c)
                    S.op("act", lambda e, pt=pt, c=c: e.activation(out=SG[c][0][:, 0:W], in_=pt[:, 0:W], func=AF.Sigmoid), reads=[rpt], writes=[SG[c][1]])
                sl3, rsl3 = load_in(3)
                U = [B_[1], B_[2]]
                GV = [B_[3], B_[4]]
                for c in range(2):
                    pt, rpt = zchunk(sl3, rsl3, c)
                    S.op("act", lambda e, pt=pt, c=c: e.activation(out=U[c][0][:, 0:W], in_=pt[:, 0:W], func=AF.Gelu_apprx_tanh), reads=[rpt], writes=[U[c][1]])
                tmpb4 = ARENA[:, ascr_offs[2]:ascr_offs[2] + 4 * SCRW].rearrange("p (c t) -> p c t", c=4)
                rtmpb = [rASCR[2], rASCR[3]]
                for c in range(2):
                    pt, rpt = zchunk(sl3, rsl3, 2 + c)
                    S.op("act", lambda e, pt=pt, c=c: e.activation(out=GV[c][0][:, 0:W], in_=pt[:, 0:W], func=AF.Gelu_apprx_tanh), reads=[rpt], writes=[GV[c][1]])
                    ln_prep(GV[c], c, W, tmpb4, rtmpb)
                sqrt_preload()

                csamp = []
                for c in range(2):
                    pt, rpt = zchunk(sl2, rsl2, c)
                    sg, rsg = SG[c]
                    S.op("dve", lambda e, pt=pt, sg=sg: e.tensor_tensor(out=sg[:, 0:W], in0=pt[:, 0:W], in1=sg[:, 0:W], op=ALU.mult), reads=[rpt, rsg], writes=[rsg])
                    S.op("act", lambda e, c=c: e.activation(out=GLUB[:, c, 0:30], in_=GLU_T[:, c, :], func=AF.Copy), reads=[rGLU_Tc[c]], writes=[rGLUBc[c]])
                    S.op("act", lambda e, sg=sg, c=c: e.activation(out=GLUB[:, c, 30:30 + W], in_=sg[:, 0:W], func=AF.Copy), reads=[rsg], writes=[rGLUBc[c]])
                    S.op("dve", lambda e, sg=sg, c=c: e.tensor_copy(out=GLU_T[:, c, :], in_=sg[:, SEGW - 30:SEGW]), reads=[rsg], writes=[rGLU_Tc[c]])
                    if smp:
                        gs = sg[:, SEGW:W]
                        S.op("dve", lambda e, gs=gs, c=c: e.tensor_copy(out=NEWS[:, 2, c, :], in_=gs), reads=[rsg], writes=[rNEWS])
                        t3v = t3[:, 0:NS * 30].rearrange("p (s k) -> p s k", k=30)
                        wbc = PARAM[:, l, C_CCW + c:C_CCW + c + 60:2].unsqueeze(1).to_broadcast([128, NS, 30])
                        S.op("dve", lambda e, t3v=t3v, wbc=wbc, c=c: e.tensor_tensor(out=t3v, in0=STC[:, c, :, :], in1=wbc, op=ALU.mult), reads=[rSTC, rPARAM], writes=[rt3])
                        sm = SMALL[:, 2 + c, :]
                        S.op("dve", lambda e, t3v=t3v, sm=sm: e.tensor_reduce(out=sm, in_=t3v, axis=AX.X, op=ALU.add), reads=[rt3], writes=[rSMALL])
                        S.op("dve", lambda e, gs=gs, sm=sm, c=c: e.scalar_tensor_tensor(out=sm, in0=gs, scalar=par(l, C_CCW + 60 + c), in1=sm, op0=ALU.mult, op1=ALU.add),
                             reads=[rsg, rSMALL, rPARAM], writes=[rSMALL])

                VT = ARENA[:, ascr_offs[3]:ascr_offs[3] + 8 * 256].rearrange("p (i c) -> p i c", i=8)
                rVT = rASCR[3]
                mean_ap, rmean = B_[0]
                rstd_ap, rrstd = B_[5]
                ln_stats(GV, W, seg, tmpb4, rtmpb, mean_ap, rmean, rstd_ap, rrstd, prep_done=True)

                for c in range(2):
                    pt2, rpt2 = next_pslot()
                    mm_chunk(pt2, rpt2, lambda k, c=c: WG[:, c, :], 1, lambda k, c0, n, c=c: POOLB[:, c, c0:c0 + n], tiles, [rWG, rPOOLB])
                    S.op("act", lambda e, pt2=pt2, c=c: e.activation(out=Y[:, 2 + c, 0:W], in_=pt2[:, 0:W], func=AF.Copy, scale=par(l, C_PSC + c)), reads=[rpt2, rPARAM], writes=[rYc[2 + c]])

                for c in range(2):
                    gv, rgv = GV[c]
                    S.op("dve", lambda e, gv=gv: e.tensor_tensor(out=gv[:, 0:W], in0=gv[:, 0:W], in1=mean_ap[:, 0:W], op=ALU.subtract), reads=[rgv, rmean], writes=[rgv])
                    S.op("dve", lambda e, gv=gv: e.tensor_tensor(out=gv[:, 0:W], in0=gv[:, 0:W], in1=rstd_ap[:, 0:W], op=ALU.mult), reads=[rgv, rrstd], writes=[rgv])
                    S.op("dve", lambda e, gv=gv, c=c: e.tensor_scalar(out=gv[:, 0:W], in0=gv[:, 0:W], scalar1=par(l, C_LDG + c), scalar2=par(l, C_LDB + c), op0=ALU.mult, op1=ALU.add),
                         reads=[rgv, rPARAM], writes=[rgv])
                    S.op("act", lambda e, gv=gv, c=c: e.activation(out=VB[:, c, 0:SEGW], in_=gv[:, 0:SEGW], func=AF.Copy), reads=[rgv], writes=[rVB])
                    if smp:
                        S.op("dve", lambda e, gv=gv, c=c: e.tensor_copy(out=NEWS[:, 3, c, :], in_=gv[:, SEGW:W]), reads=[rgv], writes=[rNEWS])

                XC = [B_[6], B_[0]]

                def conv(c):
                    pt2, rpt2 = next_pslot()

                    def cv(e):
                        ins = None
                        for (c0, n) in [(0, 512), (512, 512)]:
                            for k in range(31):
                                ins = e.matmul(pt2[:, c0:c0 + n], lhsT=DIAG[:, k, :], rhs=GLUB[:, c, c0 + k:c0 + k + n], start=(k == 0), stop=(k == 30))
                        return ins
                    S.op("pe", cv, reads=[rDIAG, rGLUBc[c]], writes=[rpt2])
                    xc, rxc = XC[c]
                    S.op("act", lambda e: e.activation(out=xc[:, 0:SEGW], in_=pt2[:, 0:SEGW], func=AF.Identity, bias=par(l, C_CCB + c), scale=1.0),
                         reads=[rpt2, rPARAM], writes=[rxc])
                    if smp:
                        sm = SMALL[:, 2 + c, :]
                        S.op("dve", lambda e: e.tensor_scalar(out=xc[:, SEGW:W], in0=sm, scalar1=par(l, C_CCB + c), scalar2=None, op0=ALU.add),
                             reads=[rSMALL, rPARAM], writes=[rxc])
                conv(0)

                PCb = PC[:, :].bitcast(BF16)
                PDb = PD[:, :].bitcast(BF16)
                for half in range(2):
                    pcb, rpcb = (PCb, rPC) if half == 0 else (PDb, rPD)

                    def trv(e, half=half, pcb=pcb):
                        ins = None
                        for i in range(4):
                            for c in range(2):
                                tbk = half * 4 + i
                                ins = e.transpose(pcb[:, i * 256 + c * 128:i * 256 + (c + 1) * 128], VB[:, c, tbk * 128:(tbk + 1) * 128], IDB[:, :])
                        return ins
                    S.op("pe", trv, reads=[rVB, rCONST], writes=[rpcb])
                    S.op("act", lambda e, half=half, pcb=pcb: e.activation(out=VT[:, half * 4:(half + 1) * 4, :], in_=pcb[:, 0:1024].rearrange("p (i c) -> p i c", i=4), func=AF.Copy),
                         reads=[rpcb], writes=[rVT])
                build_diag(l, 1)
                for c in range(2):
                    pt, rpt = next_pslot()

                    def gate(e, c=c, pt=pt):
                        ins = None
                        for i in range(8):
                            for hf in range(2):
                                hd = 2 * c + hf
                                ins = e.matmul(pt[hf * 64:(hf + 1) * 64, i * 128:(i + 1) * 128], lhsT=VT[:, i, hd * 64:(hd + 1) * 64], rhs=WST[:, hd, :], start=True, stop=True)
                        return ins
                    S.op("pe", gate, reads=[rVT, rWST], writes=[rpt])
                    gv, rgv = GV[c]
                    u, ru = U[c]
                    bsb_bc = BSB[:, c, :].unsqueeze(1).to_broadcast([128, 8, 128])
                    S.op("dve", lambda e, pt=pt, gv=gv, bsb_bc=bsb_bc: e.tensor_tensor(out=gv[:, 0:SEGW].rearrange("p (i t) -> p i t", i=8), in0=pt[:, 0:SEGW].rearrange("p (i t) -> p i t", i=8),
                                                                                    in1=bsb_bc, op=ALU.add), reads=[rpt, rBSB, rgv], writes=[rgv])
                    if smp:
                        S.op("dve", lambda e, gv=gv, c=c: e.tensor_scalar(out=gv[:, SEGW:W], in0=gv[:, SEGW:W], scalar1=WS00[:, c:c + 1], scalar2=BSB[:, c, 0:1], op0=ALU.mult, op1=ALU.add),
                             reads=[rgv, rWS00, rBSB], writes=[rgv])
                    S.op("dve", lambda e, gv=gv, u=u, c=c: e.tensor_tensor(out=Y[:, 6 + c, 0:W], in0=gv[:, 0:W], in1=u[:, 0:W], op=ALU.mult), reads=[rgv, ru], writes=[rYc[6 + c]])
                rcpC = [Res("LNCcp0"), Res("LNCcp1")]
                ln_prep(XC[0], 0, W, tmpb4, rtmpb, rcp=rcpC)
                conv(1)
                ln_prep(XC[1], 1, W, tmpb4, rtmpb, rcp=rcpC)
                sqrt_preload()

                meanc_ap, rmeanc = B_[5]
                rstdc_ap, rrstdc = B_[1]
                ln_stats(XC, W, seg, tmpb4, rtmpb, meanc_ap, rmeanc, rstdc_ap, rrstdc, prep_done=True, do_recip=False, rcp=rcpC)
                for h, (a, b) in enumerate(((0, 512), (512, W))):
                    S.op("dve", lambda e, a=a, b=b: e.reciprocal(out=rstdc_ap[:, a:b], in_=rstdc_ap[:, a:b]), reads=[rrstdc], writes=[rrstdc])
                    for c in range(2):
                        xc, rxc = XC[c]
                        S.op("dve", lambda e, xc=xc, a=a, b=b: e.tensor_tensor(out=xc[:, a:b], in0=xc[:, a:b], in1=meanc_ap[:, a:b], op=ALU.subtract), reads=[rxc, rmeanc], writes=[rxc])
                        S.op("dve", lambda e, xc=xc, a=a, b=b: e.tensor_tensor(out=xc[:, a:b], in0=xc[:, a:b], in1=rstdc_ap[:, a:b], op=ALU.mult), reads=[rxc, rrstdc], writes=[rxc])
                        S.op("act", lambda e, xc=xc, c=c, a=a, b=b: e.activation(out=Y[:, 4 + c, a:b], in_=xc[:, a:b], func=AF.Silu, bias=par(l, C_LCB + c), scale=par(l, C_LCG + c)),
                             reads=[rxc, rPARAM], writes=[rY45[c][h]])

                wo = w_out_d[l].rearrange("(k p) n -> p k n", p=128)
                slo = []
                for q in range(2):
                    sl_, rsl_ = ring_load([(lambda sl: sl[:, :].rearrange("p (k n) -> p k n", k=8), wo[:, :, q * 512:(q + 1) * 512])])
                    slo.append((sl_[:, :].rearrange("p (k n) -> p k n", k=8), rsl_))
                Rap, rR = scr_all[0]
                rRh = [rR, rR2]
                fbufs = [B_[2], B_[3], B_[4], B_[7]]
                fsq = sq_bufs(fbufs)
                fbres = [b[1] for b in fbufs]
                rFSQ = [[Res("FSQ%d_%d" % (c, h)) for h in range(2)] for c in range(8)]
                colr = [(0, 512), (512, W)]
                htiles = [[(0, 512)], [t for t in tiles if t[0] >= 512]]
                sqrt_preload()

                def oproj_part(pt, rpt, sv, j, ks, first, last, reads, tl):
                    def f(e):
                        ins = None
                        for (c0, n) in tl:
                            for i, k in enumerate(ks):
                                ins = e.matmul(pt[:, c0:c0 + n], lhsT=sv[:, k, j * 128:(j + 1) * 128], rhs=Y[:, k, c0:c0 + n],
                                               start=(first and i == 0), stop=(last and i == len(ks) - 1))
                        return ins
                    S.op("pe", f, reads=reads, writes=[rpt])

                def evac(pt, rpt, c, h):
                    a, b = colr[h]
                    S.op("dve", lambda e: e.tensor_tensor(out=X[:, c, xoff + a:xoff + b], in0=pt[:, a:b], in1=X[:, c, xoff + a:xoff + b], op=ALU.add),
                         reads=[rpt, rX[c]], writes=[rX[c], rXhs[seg][c][h]])
                    S.op("act", lambda e: e.activation(out=fsq[c][0][:, a:b], in_=X[:, c, xoff + a:xoff + b], func=AF.Square),
                         reads=[rXhs[seg][c][h]], writes=[rFSQ[c][h]])

                def ffn_norm_rest(h):
                    a, b = colr[h]
                    tl = htiles[h]
                    pt, rpt = next_pslot()

                    def f1(e):
                        ins = None
                        for (c0, n) in tl:
                            for k in range(7):
                                ins = e.matmul(pt[:, c0:c0 + n], lhsT=ONESB[:, :], rhs=fsq[k][0][:, c0:c0 + n], start=(k == 0), stop=False)
                        return ins

                    def f2(e):
                        ins = None
                        for (c0, n) in tl:
                            ins = e.matmul(pt[:, c0:c0 + n], lhsT=ONESB[:, :], rhs=fsq[7][0][:, c0:c0 + n], start=False, stop=True)
                        return ins
                    S.op("pe", f1, reads=[rFSQ[k][h] for k in range(7)] + [rCONST], writes=[rpt])
                    S.op("pe", f2, reads=[rFSQ[7][h], rCONST] + (fbres if h == 1 else []), writes=[rpt])
                    S.op("act", lambda e: e.activation(out=Rap[:, a:b], in_=pt[:, a:b], func=AF.Sqrt, bias=EPS_T[:, 0:1], scale=1.0 / D), reads=[rpt, rCONST], writes=[rRh[h]])
                    dq = []
                    dq.append(lambda: S.op("dve", lambda e: e.reciprocal(out=Rap[:, a:b], in_=Rap[:, a:b]), reads=[rRh[h]], writes=[rRh[h]]))
                    for c in range(8):
                        dq.append(lambda c=c: S.op("dve", lambda e: e.scalar_tensor_tensor(out=H[:, c, a:b], in0=X[:, c, xoff + a:xoff + b], scalar=par(l, C_NF + c), in1=Rap[:, a:b],
                                                                                         op0=ALU.mult, op1=ALU.mult), reads=[rXhs[seg][c][h], rRh[h], rPARAM], writes=[rHch[c][h]]))
                    return dq

                S.transfer(fbres, [r for pair in rFSQ for r in pair] + fbres)
                early = [0, 1, 2, 3, 6, 7]
                rY_early = [rYc[k] for k in early]

                def evac_full(pt, rpt, c):
                    S.op("dve", lambda e: e.tensor_tensor(out=X[:, c, xoff:xoff + W], in0=pt[:, 0:W], in1=X[:, c, xoff:xoff + W], op=ALU.add),
                         reads=[rpt, rX[c]], writes=[rX[c], rXhs[seg][c][0], rXhs[seg][c][1]])
                    S.op("act", lambda e: e.activation(out=fsq[c][0][:, 0:W], in_=X[:, c, xoff:xoff + W], func=AF.Square),
                         reads=[rXhs[seg][c][0], rXhs[seg][c][1]], writes=[rFSQ[c][0], rFSQ[c][1]])
                sv, rslo = slo[0]
                p0 = next_pslot()
                p1 = next_pslot()
                oproj_part(p0[0], p0[1], sv, 0, early, True, False, [rslo] + rY_early, tiles)
                oproj_part(p1[0], p1[1], sv, 1, early, True, False, [rslo] + rY_early, tiles)
                oproj_part(p0[0], p0[1], sv, 0, [4, 5], False, True, [rslo, rY45[0][0], rY45[1][0]], htiles[0])
                oproj_part(p1[0], p1[1], sv, 1, [4, 5], False, True, [rslo, rY45[0][0], rY45[1][0]], htiles[0])
                dq0 = []
                for c in range(2, 8):
                    sv, rslo = slo[c // 4]
                    j = c % 4
                    pt, rpt = (PC, rPC) if c % 2 == 0 else (PD, rPD)
                    mm_chunk(pt, rpt, lambda k, sv=sv, j=j: sv[:, k, j * 128:(j + 1) * 128], 8, lambda k, c0, n: Y[:, k, c0:c0 + n], htiles[0],
                             [rslo] + rY_early + [rY45[0][0], rY45[1][0]])
                    evac(pt, rpt, c, 0)
                sv, rslo = slo[0]
                oproj_part(p0[0], p0[1], sv, 0, [4, 5], False, True, [rslo, rY45[0][1], rY45[1][1]], htiles[1])
                evac_full(p0[0], p0[1], 0)
                oproj_part(p1[0], p1[1], sv, 1, [4, 5], False, True, [rslo, rY45[0][1], rY45[1][1]], htiles[1])
                evac_full(p1[0], p1[1], 1)
                for c in range(2, 8):
                    sv, rslo = slo[c // 4]
                    j = c % 4
                    pt, rpt = next_pslot()
                    mm_chunk(pt, rpt, lambda k, sv=sv, j=j: sv[:, k, j * 128:(j + 1) * 128], 8, lambda k, c0, n: Y[:, k, c0:c0 + n], htiles[1],
                             [rslo] + rY_early + [rY45[0][1], rY45[1][1]])
                    evac(pt, rpt, c, 1)
                    if c == 2:
                        dq0 = ffn_norm_rest(0)
                    if c >= 3:
                        for _ in range(2):
                            if dq0:
                                dq0.pop(0)()
                while dq0:
                    dq0.pop(0)()
                for f_ in ffn_norm_rest(1):
                    f_()
                S.transfer([rR, rR2], [rR])

                obatches = []
                if smp:
                    for c in range(2):
                        cs = slice(c * 128, (c + 1) * 128)
                        obatches.append(lambda c=c, cs=cs: tr_out(CA_T[:, c, :], 2, pa_o[l, :, cs], [rCA_Tc[c]]))
                        obatches.append(lambda c=c, cs=cs: tr_out(P_T[:, c, :], 15, pp_o[l, :, cs], [rP_Tc[c]]))
                        obatches.append(lambda c=c, cs=cs: tr_out(GLU_T[:, c, :], 30, pc_o[l, :, cs], [rGLU_Tc[c]]))
                        obatches.append(lambda c=c, cs=cs: tr_out(NEWS[:, 0, c, :], NS, sa_o[l, :, 1, cs], [rNEWS]))
                        obatches.append(lambda c=c, cs=cs: tr_out(NEWS[:, 1, c, :], NS, sp_o[l, :, 14, cs], [rNEWS]))
                        obatches.append(lambda c=c, cs=cs: tr_out(NEWS[:, 2, c, :], NS, sc_o[l, :, 29, cs], [rNEWS]))
                        obatches.append(lambda c=c, cs=cs: tr_out(NEWS[:, 3, c, :], NS, sv_o[l, :, cs], [rNEWS]))
                S.transfer(arena_mix, rACTBc)
                first_up = [True]
                wu = w_up_d[l].rearrange("(k p) n -> p k n", p=128)
                UG, UA, TG0, TA = scr_all[1], scr_all[2], scr_all[3], scr_all[4]
                DIAGF = DIAG[:, :, :].rearrange("p k n -> p (k n)")[:, 0:2 * SCRW].bitcast(F32)
                TGs = [TG0, (DIAGF, rDIAG)]

                def ups_out(qq):
                    g0, a0 = 2 * qq, 22 + 2 * qq
                    tr_out_multi([UPS[:, g0, :], UPS[:, g0 + 1, :], UPS[:, a0, :], UPS[:, a0 + 1, :]], NS,
                                 [(sf_o[l, :, 1, g0 * 128:(g0 + 2) * 128], 0, 256), (sf_o[l, :, 1, a0 * 128:(a0 + 2) * 128], 256, 256)],
                                 [rUPSc[g0], rUPSc[g0 + 1], rUPSc[a0], rUPSc[a0 + 1]])
                for q in range(11):
                    slu, rslu = ring_load([
                        (lambda sl: sl[:, :].rearrange("p (k n) -> p k n", k=8)[:, :, 0:256], wu[:, :, q * 256:(q + 1) * 256]),
                        (lambda sl: sl[:, :].rearrange("p (k n) -> p k n", k=8)[:, :, 256:512], wu[:, :, DFF + q * 256:DFF + (q + 1) * 256]),
                    ])
                    sv = slu[:, :].rearrange("p (k n) -> p k n", k=8)
                    for jj in range(2):
                        j = q * 2 + jj
                        outs = []
                        TG = TGs[j % 2]
                        pre = None
                        if first_up[0]:
                            first_up[0] = False
                            pre = [next_pslot(), next_pslot()]
                            for wh in range(2):
                                mm_chunk(pre[wh][0], pre[wh][1], lambda k, sv=sv, jj=jj, wh=wh: sv[:, k, wh * 256 + jj * 128:wh * 256 + (jj + 1) * 128], 8,
                                         lambda k, c0, n: H[:, k, c0:c0 + n], [tiles[0]], [rslu] + rH0)
                        for which, (ub, rub), (tbuf, rtb_) in ((0, UG, TG), (1, UA, TA)):
                            cc = j + which * 22
                            if pre is not None:
                                pt, rpt = pre[which]
                                mm_chunk(pt, rpt, lambda k, sv=sv, jj=jj, which=which: sv[:, k, which * 256 + jj * 128:which * 256 + (jj + 1) * 128], 8,
                                         lambda k, c0, n: H[:, k, c0:c0 + n], tiles[1:], [rslu] + rH1)
                            else:
                                pt, rpt = next_pslot()
                                mm_chunk(pt, rpt, lambda k, sv=sv, jj=jj, which=which: sv[:, k, which * 256 + jj * 128:which * 256 + (jj + 1) * 128], 8,
                                         lambda k, c0, n: H[:, k, c0:c0 + n], tiles, [rslu] + rHc)
                            w0, w1, w2 = (par(l, C_CFW + kk * NFC + cc) for kk in range(3))
                            S.op("act", lambda e, ub=ub, cc=cc: e.activation(out=ub[:, 0:2], in_=UP_T[:, :, cc], func=AF.Copy), reads=[rUP_T], writes=[rub])
                            S.op("act", lambda e, ub=ub, pt=pt: e.activation(out=ub[:, 2:2 + W], in_=pt[:, 0:W], func=AF.Copy), reads=[rpt], writes=[rub])
                            S.op("act", lambda e, tbuf=tbuf, pt=pt, w2=w2: e.activation(out=tbuf[:, 0:W], in_=pt[:, 0:W], func=AF.Copy, scale=w2), reads=[rpt, rPARAM], writes=[rtb_])
                            S.op("act", lambda e, pt=pt, cc=cc: e.activation(out=UP_T[:, :, cc], in_=pt[:, SEGW - 2:SEGW], func=AF.Copy), reads=[rpt], writes=[rUP_T])
                            S.op("dve", lambda e, ub=ub, tbuf=tbuf, w1=w1: e.scalar_tensor_tensor(out=tbuf[:, 0:SEGW], in0=ub[:, 1:1 + SEGW], scalar=w1, in1=tbuf[:, 0:SEGW], op0=ALU.mult, op1=ALU.add),
                                 reads=[rub, rPARAM, rtb_], writes=[rtb_])
                            S.op("dve", lambda e, ub=ub, tbuf=tbuf, w0=w0: e.scalar_tensor_tensor(out=tbuf[:, 0:SEGW], in0=ub[:, 0:SEGW], scalar=w0, in1=tbuf[:, 0:SEGW], op0=ALU.mult, op1=ALU.add),
                                 reads=[rub, rPARAM, rtb_], writes=[rtb_])
                            if smp:
                                us = ub[:, 2 + SEGW:2 + W]
                                S.op("dve", lambda e, tbuf=tbuf, w1=w1, cc=cc: e.scalar_tensor_tensor(out=tbuf[:, SEGW:W], in0=STF[:, cc, :, 1], scalar=w1, in1=tbuf[:, SEGW:W], op0=ALU.mult, op1=ALU.add),
                                     reads=[rSTF, rPARAM, rtb_], writes=[rtb_])
                                S.op("dve", lambda e, tbuf=tbuf, w0=w0, cc=cc: e.scalar_tensor_tensor(out=tbuf[:, SEGW:W], in0=STF[:, cc, :, 0], scalar=w0, in1=tbuf[:, SEGW:W], op0=ALU.mult, op1=ALU.add),
                                     reads=[rSTF, rPARAM, rtb_], writes=[rtb_])
                                S.op("dve", lambda e, us=us, cc=cc: e.tensor_copy(out=UPS[:, cc, :], in_=us), reads=[rub], writes=[rUPSc[cc]])
                        tg, rtg = TG
                        ta, rta = TA
                        S.op("act", lambda e, tg=tg: e.activation(out=tg[:, 0:W], in_=tg[:, 0:W], func=AF.Silu), reads=[rtg], writes=[rtg])
                        S.op("dve", lambda e, tg=tg, ta=ta, j=j: e.tensor_tensor(out=ACTB[:, j, 0:W], in0=tg[:, 0:W], in1=ta[:, 0:W], op=ALU.mult), reads=[rtg, rta], writes=[rACTBc[j]])
                    if seg == 0 and q < len(sbatches):
                        sbatches[q]()
                    if smp:
                        if q >= 1:
                            ups_out(q - 1)
                        if obatches and q >= 1:
                            obatches.pop(0)()
                    if seg == 1 and q == 1 and l + 1 < DEPTH:
                        prep_layer_dma(l + 1, with_wsf=False)
                    if seg == 0 and q == 10 and l + 1 < DEPTH:
                        load_params_dma(l + 1)
                        wsf_dma_hi(l + 1)
                    if seg == 1 and q == 0 and l + 1 < DEPTH:
                        prep_layer_compute(l + 1, hi=True)
                        load_params_compute(l + 1, ps=(PC, rPC))
                nxt = (l, 1) if seg == 0 else ((l + 1, 0) if l + 1 < DEPTH else None)
                if nxt is not None:
                    mixer_norm(nxt[0], nxt[1], phase="sq")
                    hoisted.add(nxt)
                wd = w_down_d[l].rearrange("(k p) n -> p k n", p=128)
                hq = []
                for c in range(8):
                    if c == 2 and nxt is not None:
                        hq = mixer_norm(nxt[0], nxt[1], phase="rest")
                    sld, rsld = ring_load([(lambda sl: sl[:, 0:22 * 128].rearrange("p (k n) -> p k n", k=22), wd[:, :, c * 128:(c + 1) * 128])])
                    sv = sld[:, 0:22 * 128].rearrange("p (k n) -> p k n", k=22)
                    pt, rpt = next_pslot()
                    if c == 0:
                        def dpart(ks, first, last, reads, pt=pt, rpt=rpt, sv=sv):
                            def f(e):
                                ins = None
                                for (c0, n) in tiles:
                                    for i, k in enumerate(ks):
                                        ins = e.matmul(pt[:, c0:c0 + n], lhsT=sv[:, k, :], rhs=ACTB[:, k, c0:c0 + n], start=(first and i == 0), stop=(last and i == len(ks) - 1))
                                return ins
                            S.op("pe", f, reads=reads, writes=[rpt])
                        dpart(list(range(20)), True, False, [rsld] + rACTBc[0:20])
                        dpart([20, 21], False, True, [rsld] + rACTBc[20:22])
                    else:
                        mm_chunk(pt, rpt, lambda k, sv=sv: sv[:, k, :], 22, lambda k, c0, n: ACTB[:, k, c0:c0 + n], tiles, [rsld] + rACTBc)
                    S.op("dve", lambda e, pt=pt, c=c: e.tensor_tensor(out=X[:, c, xoff:xoff + W], in0=pt[:, 0:W], in1=X[:, c, xoff:xoff + W], op=ALU.add), reads=[rpt, rX[c]], writes=[rX[c], rXhs[seg][c][0], rXhs[seg][c][1]])
                    for _ in range(5):
                        if hq:
                            hq.pop(0)()
                    if smp and c == 1:
                        ups_out(10)
                    if smp and c >= 2:
                        if obatches:
                            obatches.pop(0)()
                while hq:
                    hq.pop(0)()
                assert not obatches

            for seg_ in range(2):
                do_seg(seg_)
            for r in range(2):
                tr_out(UP_T[:, r, :], NFC, pf_o[l, r, :].rearrange("(c p) -> c p", p=128), [rUP_T], eng="dve")

        for l_ in range(DEPTH):
            do_layer(l_)

        rXNc = [Res("XN%d" % c) for c in range(8)]

        def fin_seg(seg):
            rX = rXs[seg]
            W = SEGW + (NS if seg == 1 else 0)
            xoff = seg * SEGW
            S.transfer(rACTBc + arena_mix + rXNc, rXNc)
            Rap, rR = scr_all[0]
            rms_stats(seg, W, xoff, sq_bufs([scr_all[1], scr_all[2], scr_all[3], scr_all[4]]), Rap, [rR, rR2])
            S.op("dve", lambda e: e.reciprocal(out=Rap[:, 0:W], in_=Rap[:, 0:W]), reads=[rR, rR2], writes=[rR, rR2])
            S.transfer([rR, rR2], [rR])
            XN = ARENA[:, 0:2 * 8 * 1040].bitcast(F32).rearrange("p (c t) -> p c t", c=8)
            for c in range(8):
                S.op("dve", lambda e, c=c: e.scalar_tensor_tensor(out=XN[:, c, 0:W], in0=X[:, c, xoff:xoff + W], scalar=NFIN[:, c:c + 1],
                                                                  in1=Rap[:, 0:W], op0=ALU.mult, op1=ALU.mult), reads=[rX[c], rR, rNFIN], writes=[rXNc[c]])
            nblk = 8 + (1 if seg == 1 else 0)
            for tb in range(nblk):
                ntok = 128 if tb < 8 else NS
                c0 = tb * 128
                pt, rpt = next_pslot()

                def trx(e, pt=pt, ntok=ntok, c0=c0):
                    ins = None
                    for c in range(8):
                        ins = e.transpose(pt[0:ntok, c * 128:(c + 1) * 128], XN[:, c, c0:c0 + ntok], IDF[:, :])
                    return ins
                S.op("pe", trx, reads=rXNc + [rCONST], writes=[rpt])
                ob, rob = scr_all[1 + tb % 4]
                if tb % 2 == 0:
                    S.op("act", lambda e, ob=ob, pt=pt, ntok=ntok: e.activation(out=ob[0:ntok, 0:1024], in_=pt[0:ntok, 0:1024], func=AF.Copy), reads=[rpt], writes=[rob])
                else:
                    S.op("dve", lambda e, ob=ob, pt=pt, ntok=ntok: e.tensor_copy(out=ob[0:ntok, 0:1024], in_=pt[0:ntok, 0:1024]), reads=[rpt], writes=[rob])
                if tb < 8:
                    dst = yp_o[xoff + c0:xoff + c0 + 128, :]
                else:
                    dst = ys_o[:, :]
                S.dma("sp", lambda e, ob=ob, dst=dst, ntok=ntok: e.dma_start(out=dst, in_=ob[0:ntok, 0:1024]), reads=[rob])

        for seg_ in range(2):
            fin_seg(seg_)
        S.finish()

        with nc.Block() as block:
            @block.tensor
            def _(e):
                for f in S.lists["pe"]:
                    f(e)

            @block.scalar
            def _(e):
                for f in S.lists["act"]:
                    f(e)

            @block.vector
            def _(e):
                for f in S.lists["dve"]:
                    f(e)

            @block.gpsimd
            def _(e):
                for f in S.lists["pool"]:
                    f(e)

            @block.sync
            def _(e):
                for f in S.lists["sp"]:
                    f(e)
    return nc


_NC_CACHE = {}


def kernel(**inputs):
    inp = {k: np.ascontiguousarray(np.asarray(v)) for k, v in inputs.items()}
    if "nc" not in _NC_CACHE:
        _NC_CACHE["nc"] = build_nc()
    nc = _NC_CACHE["nc"]
    shared = ["norm_mix", "w_in", "conv_a_w", "pool_w", "pool_scale", "conv_c_w", "conv_c_b", "ln_c_g", "ln_c_b", "ln_d_g", "ln_d_b",
              "w_s", "b_s", "w_out", "norm_ffn", "w_up", "conv_f_w", "w_down", "norm_final"]
    in_maps = []
    for i in range(NCORES):
        m = {k: inp[k] for k in shared}
        m["x_prompt"] = np.ascontiguousarray(inp["x_prompt"][i])
        m["x_sample"] = np.ascontiguousarray(inp["x_sample"][i * NS:(i + 1) * NS, 0, :])
        for k in ("state_conv_a", "state_pool", "state_conv_c", "state_conv_ffn"):
            m[k] = np.ascontiguousarray(inp[k][:, i * NS:(i + 1) * NS])
        in_maps.append(m)
    res = run_bass_kernel_spmd(nc, in_maps, core_ids=list(range(NCORES)))
    R = res.results
    y_prompt = np.stack([R[i]["y_prompt"] for i in range(NCORES)], axis=0)
    y_sample = np.concatenate([R[i]["y_sample"] for i in range(NCORES)], axis=0)[:, None, :]

    def pstack(name):
        return np.stack([R[i][name] for i in range(NCORES)], axis=1)

    def scat(name):
        return np.concatenate([R[i][name] for i in range(NCORES)], axis=1)
    outs = (y_prompt, y_sample,
            pstack("new_conv_a_prompt"), pstack("new_pool_prompt"), pstack("new_conv_c_prompt"), pstack("new_conv_ffn_prompt"),
            scat("new_conv_a_sample"), scat("new_pool_sample"), scat("new_conv_c_sample"), scat("new_conv_ffn_sample"),
            scat("new_chunk_v_sample")[:, :, None, :])
    return tuple(np.ascontiguousarray(o.astype(np.float32)) for o in outs)
```
